# Optimizing a Trainium2 kernel written in Bass

```python
import math
import jax, jax.numpy as jnp
from jax import lax
import numpy as np

D_MODEL = 1024
BATCH = 16
SEQ = 256
DEPTH = 4
DEC_BATCH = 8
DEC_SEQ = 2048
PAST_LEN = 256

GRID_W = 64
N_MIXERS = 4
NORM_EPS = 1e-6
D_FF = 2816
HY_ORDER = 2
HY_EMB = 33
HY_FILTER_W = 64
HY_SHORT_K = 3
HY_DECAY_TARGET = 1e-2
HY_FAST_PCT = 0.3
HY_SLOW_PCT = 1.5
CF_K = 31
SC_K = 3
MLA_HEADS = 8
MLA_Q_RANK = 384
MLA_KV_RANK = 256
MLA_NOPE = 128
MLA_ROPE = 64
MLA_V = 128
ROPE_THETA = 10000.0
Q_BLOCK = 128

kernel_name = 'hybrid_diffusion_trunk_step'


def _n_of(m):
    return (DEPTH - m + N_MIXERS - 1) // N_MIXERS


def rms_norm(x, g):
    x32 = x.astype(jnp.float32)
    y = x32 * lax.rsqrt(jnp.mean(x32 * x32, axis=-1, keepdims=True) + NORM_EPS)
    return y.astype(x.dtype) * g


def layer_norm(x, g, b):
    x32 = x.astype(jnp.float32)
    mu = jnp.mean(x32, axis=-1, keepdims=True)
    var = jnp.mean(jnp.square(x32 - mu), axis=-1, keepdims=True)
    return ((x32 - mu) * lax.rsqrt(var + NORM_EPS)).astype(x.dtype) * g + b


def dwconv(x, w, b=None):
    k = w.shape[0]
    y = lax.conv_general_dilated(x, w[:, None, :].astype(x.dtype), window_strides=(1,),
                                 padding=[(k // 2, k // 2)],
                                 dimension_numbers=('NWC', 'WIO', 'NWC'),
                                 feature_group_count=x.shape[-1])
    return y if b is None else y + b


def modulation(cond, w, b):
    m = jax.nn.silu(cond) @ w + b
    m = m.reshape(m.shape[0], 1, 6, D_MODEL)
    return tuple(m[:, :, k] for k in range(6))


def swiglu(h, wg, wu, wd):
    return (jax.nn.silu(h @ wg) * (h @ wu)) @ wd


def hyena_filters(seq_len, w1, b1, freq, w2, b2, w3):
    f32 = jnp.float32
    t = jnp.linspace(0.0, 1.0, seq_len, dtype=f32)[:, None]
    bands = (HY_EMB - 1) // 2
    w = 2.0 * math.pi * jnp.arange(seq_len, dtype=f32)[:, None] / seq_len
    f = jnp.linspace(1e-4, bands - 1, bands, dtype=f32)[None, :]
    z = jnp.concatenate([t, jnp.cos(f * w), -jnp.sin(f * w)], axis=-1).astype(w1.dtype)
    h = jnp.sin(freq[0] * (z @ w1 + b1))
    h = jnp.sin(freq[1] * (h @ w2 + b2))
    h = (h @ w3).reshape(seq_len, 2, HY_ORDER, D_MODEL)
    deltas = jnp.linspace(math.log(HY_DECAY_TARGET) / HY_SLOW_PCT,
                          math.log(HY_DECAY_TARGET) / HY_FAST_PCT, D_MODEL, dtype=f32)
    decay = jnp.exp(-t * jnp.abs(deltas))
    h = h.astype(f32) * decay[:, None, None, :]
    hf, hb = h[:, 0], h[:, 1]
    hc = jnp.concatenate([hf[:1] + hb[:1], hf[1:], jnp.zeros_like(hf[:1]), hb[:0:-1]], axis=0)
    return jnp.fft.rfft(hc, axis=0)


def long_conv(z, hfreq, skip):
    seq_len = z.shape[1]
    z32 = z.astype(jnp.float32)
    zf = jnp.fft.rfft(z32, n=2 * seq_len, axis=1)
    y = jnp.fft.irfft(zf * hfreq[None], n=2 * seq_len, axis=1)[:, :seq_len]
    return (y + z32 * skip.astype(jnp.float32)).astype(z.dtype)


def hyena(h, P, j):
    seq_len = h.shape[1]
    p = dwconv(h @ P['hy_w_in'][j] + P['hy_b_in'][j], P['hy_conv_w'][j], P['hy_conv_b'][j])
    v, x1, x2 = jnp.split(p, 3, axis=-1)
    hfreq = hyena_filters(seq_len, P['hy_f_w1'][j], P['hy_f_b1'][j], P['hy_f_freq'][j],
                          P['hy_f_w2'][j], P['hy_f_b2'][j], P['hy_f_w3'][j])
    skip = P['hy_skip'][j]
    zz = x1 * long_conv(v, hfreq[:, 0], skip[0])
    zz = x2 * long_conv(zz, hfreq[:, 1], skip[1])
    return zz @ P['hy_w_out'][j] + P['hy_b_out'][j]


def conformer_conv(h, P, j):
    a, g = jnp.split(h @ P['cf_w_pw1'][j] + P['cf_b_pw1'][j], 2, axis=-1)
    u = a * jax.nn.sigmoid(g)
    u = dwconv(u, P['cf_dw_w'][j], P['cf_dw_b'][j])
    u = jax.nn.silu(layer_norm(u, P['cf_ln_g'][j], P['cf_ln_b'][j]))
    return u @ P['cf_w_pw2'][j] + P['cf_b_pw2'][j]


def short_conv(h, P, j):
    bg, cg, hv = jnp.split(h @ P['sc_w_in'][j], 3, axis=-1)
    return (bg * dwconv(cg * hv, P['sc_conv_w'][j])) @ P['sc_w_out'][j]


def rope_2d_tables(seq_len):
    f32 = jnp.float32
    rows = seq_len // GRID_W
    row = jnp.repeat(jnp.arange(rows, dtype=f32), GRID_W)
    col = jnp.tile(jnp.arange(GRID_W, dtype=f32), rows)
    nf = MLA_ROPE // 4
    inv = jnp.exp(-math.log(ROPE_THETA) * jnp.arange(nf, dtype=f32) * (4.0 / MLA_ROPE))
    ang = jnp.stack([row[:, None] * inv, col[:, None] * inv], axis=1)
    return jnp.cos(ang), jnp.sin(ang)


def apply_rope(x, cos, sin):
    nf = MLA_ROPE // 4
    xr = x.reshape(*x.shape[:-1], 2, 2, nf)
    x1, x2 = xr[..., 0, :], xr[..., 1, :]
    cos = cos.astype(x.dtype)
    sin = sin.astype(x.dtype)
    return jnp.stack([x1 * cos - x2 * sin, x1 * sin + x2 * cos], axis=-2).reshape(x.shape)


def mla_project(h, P, j):
    bsz, seq_len = h.shape[:2]
    cq = rms_norm(h @ P['mla_w_dq'][j], P['mla_g_q'][j])
    q = (cq @ P['mla_w_uq'][j]).reshape(bsz, seq_len, MLA_HEADS, MLA_NOPE + MLA_ROPE)
    kv = h @ P['mla_w_dkv'][j]
    ckv = rms_norm(kv[..., :MLA_KV_RANK], P['mla_g_kv'][j])
    kpe = kv[..., MLA_KV_RANK:]
    return q[..., :MLA_NOPE], q[..., MLA_NOPE:], ckv, kpe


def block_attention(qn, qp, kn, kp, v):
    bsz, lq = qn.shape[:2]
    nb = lq // Q_BLOCK
    scale = (MLA_NOPE + MLA_ROPE) ** -0.5

    def blocks(a):
        return jnp.moveaxis(a.reshape(bsz, nb, Q_BLOCK, *a.shape[2:]), 1, 0)

    def one(args):
        qn_b, qp_b = args
        s = (jnp.einsum('bqhd,bkhd->bhqk', qn_b, kn)
             + jnp.einsum('bqhr,bkr->bhqk', qp_b, kp))
        p = jax.nn.softmax(s.astype(jnp.float32) * scale, axis=-1).astype(v.dtype)
        return jnp.einsum('bhqk,bkhd->bqhd', p, v)

    o = lax.map(one, (blocks(qn), blocks(qp)))
    return jnp.moveaxis(o, 0, 1).reshape(bsz, lq, MLA_HEADS, MLA_V)


def mla_attend(qn, qp, ckv, kpe, P, j):
    bsz, lq = qn.shape[:2]
    lk = ckv.shape[1]
    kv = (ckv @ P['mla_w_ukv'][j]).reshape(bsz, lk, MLA_HEADS, MLA_NOPE + MLA_V)
    o = block_attention(qn, qp, kv[..., :MLA_NOPE], kpe, kv[..., MLA_NOPE:])
    return o.reshape(bsz, lq, MLA_HEADS * MLA_V) @ P['mla_w_o'][j]


def run_trunk(x, cond, P, cache):
    seq_len = x.shape[1]
    rope = None if cache is None else rope_2d_tables(seq_len)
    new_ckv, new_kpe = [], []
    for i in range(DEPTH):
        m, j = i % N_MIXERS, i // N_MIXERS
        sh1, sc1, g1, sh2, sc2, g2 = modulation(cond, P['w_mod'][i], P['b_mod'][i])
        gn = P['norm_g'][i]
        h = rms_norm(x, gn[0]) * (1 + sc1) + sh1
        if m == 0:
            h = hyena(h, P, j)
        elif m == 1:
            h = conformer_conv(h, P, j)
        elif m == 2:
            h = short_conv(h, P, j)
        else:
            qn, qp, ckv, kpe = mla_project(h, P, j)
            if cache is None:
                new_ckv.append(ckv)
                new_kpe.append(kpe)
            else:
                cos, sin = rope
                qp = apply_rope(qp, cos[:, None], sin[:, None])
                kpe = apply_rope(kpe, cos, sin)
                ckv = jnp.concatenate([cache[0][:, j], ckv], axis=1)
                kpe = jnp.concatenate([cache[1][:, j], kpe], axis=1)
            h = mla_attend(qn, qp, ckv, kpe, P, j)
        x = x + g1 * rms_norm(h, gn[1])
        h = rms_norm(x, gn[2]) * (1 + sc2) + sh2
        h = swiglu(h, P['ffn_w_gate'][i], P['ffn_w_up'][i], P['ffn_w_down'][i])
        x = x + g2 * rms_norm(h, gn[3])
    return x, new_ckv, new_kpe


def setup_inputs(seed: int = 0) -> dict:
    key = jax.random.key(seed)
    ks = iter(jax.random.split(key, 64))
    D = D_MODEL
    nA, nB, nC, nD = _n_of(0), _n_of(1), _n_of(2), _n_of(3)
    HQK = MLA_HEADS * (MLA_NOPE + MLA_ROPE)

    def nrm(shape, scale):
        return scale * jax.random.normal(next(ks), shape, jnp.float32)

    def gain(shape):
        return 1.0 + nrm(shape, 0.01)

    return {
        'x_prompt': nrm((BATCH, SEQ, D), 1.0),
        'x_sample': nrm((DEC_BATCH, DEC_SEQ, D), 1.0),
        'c': nrm((DEC_BATCH, D), 1.0),
        'cache_ckv': nrm((DEC_BATCH, nD, PAST_LEN, MLA_KV_RANK), 1.0),
        'cache_kpe': nrm((DEC_BATCH, nD, PAST_LEN, MLA_ROPE), 1.0),
        'c_ctx': nrm((D,), 1.0),
        'w_mod': nrm((DEPTH, D, 6 * D), 0.5 * D ** -0.5),
        'b_mod': nrm((DEPTH, 6 * D), 0.01),
        'norm_g': gain((DEPTH, 4, D)),
        'ffn_w_gate': nrm((DEPTH, D, D_FF), D ** -0.5),
        'ffn_w_up': nrm((DEPTH, D, D_FF), D ** -0.5),
        'ffn_w_down': nrm((DEPTH, D_FF, D), D_FF ** -0.5),
        'hy_w_in': nrm((nA, D, 3 * D), D ** -0.5),
        'hy_b_in': nrm((nA, 3 * D), 0.01),
        'hy_conv_w': nrm((nA, HY_SHORT_K, 3 * D), HY_SHORT_K ** -0.5),
        'hy_conv_b': nrm((nA, 3 * D), 0.01),
        'hy_f_w1': nrm((nA, HY_EMB, HY_FILTER_W), HY_EMB ** -0.5),
        'hy_f_b1': nrm((nA, HY_FILTER_W), 0.01),
        'hy_f_freq': gain((nA, 2, HY_FILTER_W)),
        'hy_f_w2': nrm((nA, HY_FILTER_W, HY_FILTER_W), HY_FILTER_W ** -0.5),
        'hy_f_b2': nrm((nA, HY_FILTER_W), 0.01),
        'hy_f_w3': nrm((nA, HY_FILTER_W, 2 * HY_ORDER * D), 0.05 * HY_FILTER_W ** -0.5),
        'hy_skip': nrm((nA, HY_ORDER, D), 0.1),
        'hy_w_out': nrm((nA, D, D), D ** -0.5),
        'hy_b_out': nrm((nA, D), 0.01),
        'cf_w_pw1': nrm((nB, D, 2 * D), D ** -0.5),
        'cf_b_pw1': nrm((nB, 2 * D), 0.01),
        'cf_dw_w': nrm((nB, CF_K, D), CF_K ** -0.5),
        'cf_dw_b': nrm((nB, D), 0.01),
        'cf_ln_g': gain((nB, D)),
        'cf_ln_b': nrm((nB, D), 0.01),
        'cf_w_pw2': nrm((nB, D, D), D ** -0.5),
        'cf_b_pw2': nrm((nB, D), 0.01),
        'sc_w_in': nrm((nC, D, 3 * D), D ** -0.5),
        'sc_conv_w': nrm((nC, SC_K, D), SC_K ** -0.5),
        'sc_w_out': nrm((nC, D, D), D ** -0.5),
        'mla_w_dq': nrm((nD, D, MLA_Q_RANK), D ** -0.5),
        'mla_g_q': gain((nD, MLA_Q_RANK)),
        'mla_w_uq': nrm((nD, MLA_Q_RANK, HQK), MLA_Q_RANK ** -0.5),
        'mla_w_dkv': nrm((nD, D, MLA_KV_RANK + MLA_ROPE), D ** -0.5),
        'mla_g_kv': gain((nD, MLA_KV_RANK)),
        'mla_w_ukv': nrm((nD, MLA_KV_RANK, MLA_HEADS * (MLA_NOPE + MLA_V)), MLA_KV_RANK ** -0.5),
        'mla_w_o': nrm((nD, MLA_HEADS * MLA_V, D), (MLA_HEADS * MLA_V) ** -0.5),
    }


def reference(x_prompt, x_sample, c, cache_ckv, cache_kpe, c_ctx, w_mod, b_mod, norm_g,
              ffn_w_gate, ffn_w_up, ffn_w_down,
              hy_w_in, hy_b_in, hy_conv_w, hy_conv_b, hy_f_w1, hy_f_b1, hy_f_freq,
              hy_f_w2, hy_f_b2, hy_f_w3, hy_skip, hy_w_out, hy_b_out,
              cf_w_pw1, cf_b_pw1, cf_dw_w, cf_dw_b, cf_ln_g, cf_ln_b, cf_w_pw2, cf_b_pw2,
              sc_w_in, sc_conv_w, sc_w_out,
              mla_w_dq, mla_g_q, mla_w_uq, mla_w_dkv, mla_g_kv, mla_w_ukv, mla_w_o):
    P = dict(w_mod=w_mod, b_mod=b_mod, norm_g=norm_g,
             ffn_w_gate=ffn_w_gate, ffn_w_up=ffn_w_up, ffn_w_down=ffn_w_down,
             hy_w_in=hy_w_in, hy_b_in=hy_b_in, hy_conv_w=hy_conv_w, hy_conv_b=hy_conv_b,
             hy_f_w1=hy_f_w1, hy_f_b1=hy_f_b1, hy_f_freq=hy_f_freq, hy_f_w2=hy_f_w2,
             hy_f_b2=hy_f_b2, hy_f_w3=hy_f_w3, hy_skip=hy_skip, hy_w_out=hy_w_out,
             hy_b_out=hy_b_out,
             cf_w_pw1=cf_w_pw1, cf_b_pw1=cf_b_pw1, cf_dw_w=cf_dw_w, cf_dw_b=cf_dw_b,
             cf_ln_g=cf_ln_g, cf_ln_b=cf_ln_b, cf_w_pw2=cf_w_pw2, cf_b_pw2=cf_b_pw2,
             sc_w_in=sc_w_in, sc_conv_w=sc_conv_w, sc_w_out=sc_w_out,
             mla_w_dq=mla_w_dq, mla_g_q=mla_g_q, mla_w_uq=mla_w_uq, mla_w_dkv=mla_w_dkv,
             mla_g_kv=mla_g_kv, mla_w_ukv=mla_w_ukv, mla_w_o=mla_w_o)
    y_prompt, ckv_list, kpe_list = run_trunk(x_prompt, c_ctx[None, :], P, None)
    new_ckv = jnp.stack(ckv_list, axis=1)
    new_kpe = jnp.stack(kpe_list, axis=1)
    y_sample, _, _ = run_trunk(x_sample, c, P, (cache_ckv, cache_kpe))
    return (y_prompt, y_sample, new_ckv, new_kpe)
```

```python
import math
import numpy as np
import ml_dtypes
import concourse.bass as bass
import concourse.mybir as mybir
from concourse.bass_utils import run_bass_kernel_spmd

F32 = mybir.dt.float32
BF16 = mybir.dt.bfloat16
U8 = mybir.dt.uint8
AF = mybir.ActivationFunctionType
ALU = mybir.AluOpType
AX = mybir.AxisListType

D = 1024
NCH = 8
DFF = 2816
NHC = 22
LS = 2048
LP = 256
T = 2560
PAST = 256
EPS = 1e-6
NCORES = 8
ARENA_BYTES = 206 * 1024

DEPTH_LIMIT = 4
MIXERS = [0, 1, 2, 3]
DEBUG_X = False
MLA_STAGE = 0


class Buf:
    __slots__ = ("name", "w", "r", "dsem", "dcnt", "t", "excl")

    def __init__(self, name, t=None):
        self.name = name
        self.excl = False
        self.w = None
        self.r = {}
        self.dsem = None
        self.dcnt = 0
        self.t = t

    def __getitem__(self, k):
        return self.t[k]


class Sched:
    def __init__(self, nc):
        self.nc = nc
        self.engs = {"pe": nc.tensor, "act": nc.scalar, "dve": nc.vector, "pool": nc.gpsimd, "sp": nc.sync}
        self.sems = {}
        self.cnt = {}
        self.seen = {e: {} for e in self.engs}
        for e in self.engs:
            self.sems[e] = nc.alloc_semaphore(name="prog_" + e)
            self.cnt[e] = 0
        self.ndsem = 0
        self.bufs = []

    def buf(self, name, t=None):
        b = Buf(name, t)
        self.bufs.append(b)
        return b

    def _deps(self, reads, writes, skip=None, eng=None):
        deps = {}
        for b in reads:
            if b.w is not None:
                k, v = b.w
                if k != skip and deps.get(k, 0) < v:
                    deps[k] = v
            if b.excl:
                for k, v in b.r.items():
                    if k != eng and deps.get(k, 0) < v:
                        deps[k] = v
        for b in writes:
            if b.w is not None:
                k, v = b.w
                if k != skip and deps.get(k, 0) < v:
                    deps[k] = v
            for k, v in b.r.items():
                if k != skip and deps.get(k, 0) < v:
                    deps[k] = v
        return deps

    def _wait(self, eng, deps):
        e = self.engs[eng]
        seen = self.seen[eng]
        for k, v in deps.items():
            if seen.get(k, 0) >= v:
                continue
            e.wait_ge(self.sems[k], v)
            seen[k] = v

    def _record(self, ev, reads, writes):
        k, v = ev
        for b in writes:
            b.w = ev
            b.r = {}
        for b in reads:
            if b in writes:
                continue
            if b.r.get(k, 0) < v:
                b.r[k] = v

    def op(self, eng, fn, reads=(), writes=()):
        deps = self._deps(reads, writes, "pe" if eng == "pe" else None, eng)
        self._wait(eng, deps)
        inst = fn(self.engs[eng])
        self.cnt[eng] += 1
        inst.then_inc(self.sems[eng], 1)
        ev = (eng, self.cnt[eng])
        self._record(ev, reads, writes)
        return ev

    def dma(self, eng, out, in_, reads=(), writes=(), dbuf=None):
        if dbuf is None:
            dbuf = writes[0]
        if dbuf.dsem is None:
            key = "d%d" % self.ndsem
            self.ndsem += 1
            self.sems[key] = self.nc.alloc_semaphore(name=key + "_" + dbuf.name)
            dbuf.dsem = key
        deps = self._deps(reads, writes, skip=dbuf.dsem)
        self._wait(eng, deps)
        inst = self.engs[eng].dma_start(out=out, in_=in_)
        dbuf.dcnt += 16
        inst.then_inc(self.sems[dbuf.dsem], 16)
        ev = (dbuf.dsem, dbuf.dcnt)
        self._record(ev, reads, writes)
        return ev

    def _allev(self):
        allev = {}
        for e in self.engs:
            if self.cnt[e] > 0:
                allev[e] = self.cnt[e]
        for b in self.bufs:
            if b.dsem is not None and b.dcnt > 0:
                allev[b.dsem] = b.dcnt
        return allev

    def barrier(self):
        allev = self._allev()
        for e in self.engs:
            self._wait(e, {k: v for k, v in allev.items() if k != e})

    def finish(self, eng="sp"):
        allev = self._allev()
        self._wait(eng, {k: v for k, v in allev.items() if k != eng})


class Arena:
    def __init__(self, nc, S, nbytes):
        self.S = S
        self.n = nbytes
        self.t = nc.alloc_sbuf_tensor("arena", [128, nbytes], U8)
        self.off = 0
        self.peak = 0

    def alloc(self, name, shape, dt, parts=128):
        esz = 4 if dt == F32 else 2
        n = int(np.prod(shape)) * esz
        n_al = (n + 63) // 64 * 64
        if self.off + n_al > self.n:
            raise RuntimeError("SBUF arena overflow at %s: need %d have %d" % (name, n_al, self.n - self.off))
        ap = self.t[0:parts, self.off:self.off + n].bitcast(dt)
        if len(shape) == 2:
            ap = ap.rearrange("p (a b) -> p a b", a=shape[0])
        elif len(shape) == 3:
            ap = ap.rearrange("p (a b c) -> p a b c", a=shape[0], b=shape[1])
        self.off += n_al
        self.peak = max(self.peak, self.off)
        return self.S.buf(name, ap)

    def mark(self):
        return self.off

    def release(self, mark):
        self.off = mark


VEC_LAYOUT = None


def _vec_rows(inputs):
    rows = []
    idx = {}

    def add(name, v):
        v = np.asarray(v, np.float32).reshape(-1)
        n = v.shape[0]
        nr = (n + 127) // 128
        pad = np.zeros(nr * 128, np.float32)
        pad[:n] = v
        idx[name] = (len(rows), nr)
        for r in range(nr):
            rows.append(pad[r * 128:(r + 1) * 128])

    for i in range(4):
        for k in range(4):
            add("gn%d_%d" % (i, k), inputs["norm_g"][i, k])
        add("bmod%d" % i, inputs["b_mod"][i])
    add("hy_b_in", inputs["hy_b_in"][0])
    for k in range(3):
        add("hy_cw%d" % k, inputs["hy_conv_w"][0, k])
    add("hy_cb", inputs["hy_conv_b"][0])
    add("hy_b_out", inputs["hy_b_out"][0])
    add("hy_fb1", inputs["hy_f_b1"][0])
    add("hy_fb2", inputs["hy_f_b2"][0])
    add("hy_fq0", inputs["hy_f_freq"][0, 0])
    add("hy_fq1", inputs["hy_f_freq"][0, 1])
    add("cf_b1", inputs["cf_b_pw1"][0])
    for k in range(31):
        add("cf_dw%d" % k, inputs["cf_dw_w"][0, k])
    add("cf_dwb", inputs["cf_dw_b"][0])
    add("cf_lng", inputs["cf_ln_g"][0])
    add("cf_lnb", inputs["cf_ln_b"][0])
    add("cf_b2", inputs["cf_b_pw2"][0])
    for k in range(3):
        add("sc_cw%d" % k, inputs["sc_conv_w"][0, k])
    add("gq", inputs["mla_g_q"][0])
    return rows, idx


def _vec_index():
    dummy = {
        "norm_g": np.zeros((4, 4, D)), "b_mod": np.zeros((4, 6 * D)), "hy_b_in": np.zeros((1, 3 * D)),
        "hy_conv_w": np.zeros((1, 3, 3 * D)), "hy_conv_b": np.zeros((1, 3 * D)), "hy_b_out": np.zeros((1, D)),
        "hy_f_b1": np.zeros((1, 64)), "hy_f_b2": np.zeros((1, 64)), "hy_f_freq": np.zeros((1, 2, 64)),
        "cf_b_pw1": np.zeros((1, 2 * D)), "cf_dw_w": np.zeros((1, 31, D)), "cf_dw_b": np.zeros((1, D)),
        "cf_ln_g": np.zeros((1, D)), "cf_ln_b": np.zeros((1, D)), "cf_b_pw2": np.zeros((1, D)),
        "sc_conv_w": np.zeros((1, 3, D)), "mla_g_q": np.zeros((1, 384)),
    }
    rows, idx = _vec_rows(dummy)
    return len(rows), idx


def _dft_consts(L):
    N = 2 * L
    t = np.arange(L, dtype=np.float64)
    ang = 2.0 * np.pi * np.outer(t, t) / N
    C = np.cos(ang)
    Sm = np.sin(ang)
    nyq = (-1.0) ** t
    Sf = Sm.copy()
    Sf[:, 0] = nyq
    TC = L // 128
    FC = L // 128
    F = np.stack([C, Sf])
    F = F.reshape(2, TC, 128, FC, 128).transpose(0, 3, 2, 1, 4)
    Si = Sf.T
    I = np.stack([C, Si])
    scl = np.full((128, FC), 2.0 / N, np.float32)
    scl[0, 0] = 1.0 / N
    nyqcol = nyq.reshape(TC, 128).T.copy()
    return (np.ascontiguousarray(F).astype(ml_dtypes.bfloat16), np.ascontiguousarray(I).astype(ml_dtypes.bfloat16),
            scl, nyqcol.astype(ml_dtypes.bfloat16))


def _hyena_pos(L):
    f32 = np.float32
    t = np.linspace(0.0, 1.0, L, dtype=f32)[:, None]
    bands = 16
    w = (2.0 * np.pi * np.arange(L, dtype=f32)[:, None] / L).astype(f32)
    f = np.linspace(1e-4, bands - 1, bands, dtype=f32)[None, :]
    z = np.concatenate([t, np.cos(f * w), -np.sin(f * w)], axis=-1).astype(f32)
    negt = (-t[:, 0]).reshape(L // 128, 128).T.copy()
    return np.ascontiguousarray(z.T), negt.astype(f32)


def _rope_tables():
    f32 = np.float32
    rows = LS // 64
    row = np.repeat(np.arange(rows, dtype=f32), 64)
    col = np.tile(np.arange(64, dtype=f32), rows)
    nf = 16
    inv = np.exp(-math.log(10000.0) * np.arange(nf, dtype=f32) * (4.0 / 64)).astype(f32)
    ang = np.stack([row[:, None] * inv, col[:, None] * inv], axis=1)
    cos = np.cos(ang).astype(f32)
    sin = np.sin(ang).astype(f32)
    cosK = cos.reshape(LS, 32)
    sinK = sin.reshape(LS, 32)
    cosQ = np.zeros((64, LS), f32)
    sinQ = np.zeros((64, LS), f32)
    for a in range(2):
        for s in range(2):
            for f in range(16):
                d = a * 32 + s * 16 + f
                cosQ[d] = cos[:, a, f]
                sinQ[d] = sin[:, a, f] * (-1.0 if s == 0 else 1.0)
    return cosK, sinK, cosQ, sinQ


_CONST_CACHE = {}


def _consts():
    if _CONST_CACHE:
        return _CONST_CACHE
    c = {}
    for L in (LS, LP):
        Fm, Im, scl, nyqcol = _dft_consts(L)
        c["dftF%d" % L] = Fm
        c["dftI%d" % L] = Im
        c["dscl%d" % L] = scl
        c["nyq%d" % L] = nyqcol
        zT, negt = _hyena_pos(L)
        c["zT%d" % L] = zT
        c["negt%d" % L] = negt
    deltas = np.linspace(math.log(1e-2) / 1.5, math.log(1e-2) / 0.3, D, dtype=np.float32)
    c["absdelta"] = np.abs(deltas).reshape(1, D).astype(np.float32)
    cosK, sinK, cosQ, sinQ = _rope_tables()
    c["cosK"] = cosK
    c["sinK"] = sinK
    c["cosQ"] = cosQ
    c["sinQ"] = sinQ
    c["ident"] = np.eye(128, dtype=np.float32)
    _CONST_CACHE.update(c)
    return c


WEIGHT_NAMES = ["w_mod", "ffn_w_gate", "ffn_w_up", "ffn_w_down", "hy_w_in", "hy_f_w1", "hy_f_w2", "hy_f_w3",
                "hy_skip", "hy_w_out", "cf_w_pw1", "cf_w_pw2", "sc_w_in", "sc_w_out", "mla_w_dq", "mla_w_uq",
                "mla_w_dkv", "mla_g_kv", "mla_w_ukv", "mla_w_o"]


TILES = [(0, 512, 0), (512, 512, 0), (1024, 512, 0), (1536, 512, 0), (2048, 512, 1)]
SEQS = [(0, LS, True), (LS, LP, False), (LS + LP, LP, False)]
FFN_ST = [(0, [(0, 512, 0), (512, 512, 0), (1024, 256, 0)]),
          (1280, [(1280, 512, 0), (1792, 256, 0), (2048, 512, 1)])]


class Ring:
    def __init__(self, P, name, nslots, elems):
        self.P = P
        self.slots = [P.A.alloc("%s%d" % (name, i), [elems], BF16) for i in range(nslots)]
        self.i = 0

    def next(self):
        s = self.slots[self.i % len(self.slots)]
        self.i += 1
        return s


class Prog:
    def __init__(self):
        nc = bass.Bass("TRN2", target_bir_lowering=False)
        self.nc = nc
        self.S = Sched(nc)
        self.A = Arena(nc, self.S, ARENA_BYTES)
        self.NV, self.vidx = _vec_index()
        self.NVT = self.NV + 16
        self.dr = {}
        self.build()

    def din(self, name, shape, dt=F32):
        self.dr[name] = self.nc.dram_tensor(name, list(shape), dt, kind="ExternalInput").ap()
        return self.dr[name]

    def dout(self, name, shape):
        ap = self.nc.dram_tensor(name, list(shape), F32, kind="ExternalOutput").ap()
        self.dr[name] = ap
        return ap

    def vcol(self, name, j=0, n=1):
        r0, nr = self.vidx[name]
        return self.vtab[:, r0 + j:r0 + j + n]

    def mm(self, ps, out, lhsT, rhs, start, stop, reads):
        self.S.op("pe", lambda e: e.matmul(out, lhsT, rhs, start=start, stop=stop), reads=reads, writes=[ps])

    def act(self, out, in_, func, reads, writes, bias=None, scale=None, accum_out=None):
        kw = {}
        if bias is not None:
            kw["bias"] = bias
        if scale is not None:
            kw["scale"] = scale
        if accum_out is not None:
            kw["accum_out"] = accum_out
        self.S.op("act", lambda e: e.activation(out=out, in_=in_, func=func, **kw), reads=reads, writes=writes)

    def tt(self, out, in0, in1, op, reads, writes, eng="dve"):
        self.S.op(eng, lambda e: e.tensor_tensor(out, in0, in1, op), reads=reads, writes=writes)

    def ts(self, out, in0, s1, s2, op0, op1, reads, writes, eng="dve"):
        if op1 is None:
            self.S.op(eng, lambda e: e.tensor_scalar(out, in0, s1, None, op0), reads=reads, writes=writes)
        else:
            self.S.op(eng, lambda e: e.tensor_scalar(out, in0, s1, s2, op0, op1), reads=reads, writes=writes)

    def stt(self, out, in0, scalar, in1, op0, op1, reads, writes, eng="dve"):
        self.S.op(eng, lambda e: e.scalar_tensor_tensor(out=out, in0=in0, scalar=scalar, in1=in1, op0=op0, op1=op1),
                  reads=reads, writes=writes)

    def copy(self, eng, out, in_, reads, writes):
        if eng == "act":
            self.S.op("act", lambda e: e.copy(out, in_), reads=reads, writes=writes)
        else:
            self.S.op(eng, lambda e: e.tensor_copy(out, in_), reads=reads, writes=writes)

    def wload(self, ring, src, kc, ncols, eng="pool", koff=0, ktot=None):
        slot = ring.next()
        if ktot is None:
            ktot = kc
        view = slot.t[:, 0:ktot * ncols].rearrange("p (k c) -> p k c", k=ktot)
        self.S.dma(eng, view[:, koff:koff + kc, :], src, writes=[slot])
        return slot, view

    def build(self):
        nc, S, A = self.nc, self.S, self.A
        din = self.din
        c = _consts()
        din("xs", [T, D])
        din("vecs", [self.NVT, 128])
        din("cckv", [PAST, 256])
        din("ckpe", [PAST, 64])
        din("w_mod", [4, D, 6 * D])
        din("ffn_w_gate", [4, D, DFF])
        din("ffn_w_up", [4, D, DFF])
        din("ffn_w_down", [4, DFF, D])
        din("hy_w_in", [D, 3 * D])
        din("hy_f_w1", [33, 64])
        din("hy_f_w2", [64, 64])
        din("hy_f_w3", [64, 4 * D])
        din("hy_skip", [2, D])
        din("hy_w_out", [D, D])
        din("cf_w_pw1", [D, 2 * D])
        din("cf_w_pw2", [D, D])
        din("sc_w_in", [D, 3 * D])
        din("sc_w_out", [D, D])
        din("mla_w_dq", [D, 384])
        din("mla_w_uq", [384, 1536])
        din("mla_w_uq_sw", [384, 512])
        din("mla_w_dkv", [D, 320])
        din("mla_g_kv", [1, 256])
        din("mla_w_ukv", [256, 2048])
        din("mla_w_o", [D, D])
        for k, v in c.items():
            din("c_" + k, v.shape, BF16 if v.dtype == ml_dtypes.bfloat16 else F32)
        self.y_out = self.dout("y", [T, D])
        self.ckv_out = self.dout("nckv", [2 * LP, 256])
        self.kpe_out = self.dout("nkpe", [2 * LP, 64])
        self.Y = S.buf("Ydram")
        self.CKVO = S.buf("CKVOdram")
        self.htab = {}
        self.HT = {}
        for L in (LS, LP):
            self.htab[L] = nc.dram_tensor("htab%d" % L, [2, 2, L // 128, 128, D], F32, kind="Internal").ap()
            self.HT[L] = S.buf("HT%d" % L)

        self.ps = [S.buf("ps%d" % i, nc.alloc_psum_tensor("ps%d" % i, [128, 512], F32)) for i in range(8)]
        for b in self.ps:
            b.excl = True

        self.identf = A.alloc("identf", [128], F32)
        self.identb = A.alloc("identb", [128], BF16)
        self.onesb = A.alloc("onesb", [128], BF16)
        self.vtab = None
        vt = A.alloc("vtab", [self.NVT], F32)
        self.vtabB = vt
        self.vtab = vt.t
        self.scT = A.alloc("scT", [8, 2], BF16)
        self.lt = [A.alloc("lt%d" % i, [6, 8, 2], F32) for i in range(4)]
        self.modr = A.alloc("modr", [48, 2], F32)
        self.sq = [A.alloc("sq%d" % i, [512], BF16) for i in range(2)]
        self.rt = A.alloc("rt", [512], F32)
        self.rstd = A.alloc("rstd", [512], F32)
        self.tf = [A.alloc("tf%d" % i, [512], F32) for i in range(3)]
        self.tfi = 0
        self.sqi = 0
        self.epsb = A.alloc("epsb", [1], F32)
        self.xT_off = A.mark()
        self.xT = A.alloc("xT", [NCH, T], F32)
        self.base = A.mark()

        S.dma("sp", self.identf[:], self.dr["c_ident"][:, :], writes=[self.identf])
        self.copy("act", self.identb[:], self.identf[:], [self.identf], [self.identb])
        S.op("dve", lambda e: e.memset(self.onesb[:], 1.0), writes=[self.onesb])
        S.op("dve", lambda e: e.memset(self.epsb[:], EPS), writes=[self.epsb])
        self.load_vecs()
        depth = DEPTH_LIMIT
        if not (MIXERS[0] == 0):
            self.modulation(0)
        for i in range(depth):
            m = MIXERS[i]
            if m == 0 and i == 0:
                self.hyena_layer0()
            else:
                if i == 0:
                    self.load_xT()
                if m == 0:
                    raise NotImplementedError
                elif m == 1:
                    self.conformer(i)
                elif m == 2:
                    self.shortconv(i)
                else:
                    self.mla(i)
            self.ffn(i, embed_mod=(i + 1 if i + 1 < depth else None))
        self.store_out()
        S.finish("sp")

    def next_tf(self):
        b = self.tf[self.tfi % 3]
        self.tfi += 1
        return b

    def next_sq(self):
        b = self.sq[self.sqi % 2]
        self.sqi += 1
        return b

    def load_vecs(self):
        S, A = self.S, self.A
        mk = A.mark()
        stg = A.alloc("vstg", [128], F32)
        ps = self.ps[7]
        for g in range((self.NVT + 127) // 128):
            r0 = g * 128
            nr = min(128, self.NVT - r0)
            S.dma("sp", stg[0:nr, :], self.dr["vecs"][r0:r0 + nr, :], writes=[stg])
            S.op("pe", lambda e: e.transpose(ps[:, 0:nr], stg[0:nr, :], self.identf[0:nr, 0:nr]),
                 reads=[stg, self.identf], writes=[ps])
            self.copy("dve", self.vtab[:, r0:r0 + nr], ps[:, 0:nr], [ps], [self.vtabB])
        cc = self.vtab[:, self.NV:self.NV + 16]
        self.act(self.scT.t.rearrange("p a b -> p (a b)"), cc, AF.Silu, [self.vtabB], [self.scT])
        S.barrier()
        A.release(mk)

    def modulation(self, i):
        S, A = self.S, self.A
        mk = A.mark()
        ring = Ring(self, "mring", 3, 8 * 384)
        ps = self.ps[7]
        wsrc = self.dr["w_mod"][i].rearrange("(kc p) n -> p kc n", p=128)
        for g in range(16):
            slot, view = self.wload(ring, wsrc[:, :, g * 384:(g + 1) * 384], 8, 384)
            for f in range(3):
                fc = g * 3 + f
                for kc in range(8):
                    self.mm(ps, ps[:, fc * 2:fc * 2 + 2], view[:, kc, f * 128:(f + 1) * 128], self.scT[:, kc, :],
                            kc == 0, kc == 7, [slot, self.scT])
        self.modulation_finish(i, ps)
        S.barrier()
        A.release(mk)

    def modulation_finish(self, i, ps):
        r0, _ = self.vidx["bmod%d" % i]
        bm = self.vtab[:, r0:r0 + 48]
        psv = ps[:, 0:96].rearrange("p (f c) -> p f c", c=2)
        for col in range(2):
            self.tt(self.modr[:, :, col], psv[:, :, col], bm, ALU.add, [ps, self.vtabB], [self.modr])
        lt = self.lt[i]
        g0 = self.vcol("gn%d_0" % i, 0, 8)
        g1 = self.vcol("gn%d_1" % i, 0, 8)
        g2 = self.vcol("gn%d_2" % i, 0, 8)
        g3 = self.vcol("gn%d_3" % i, 0, 8)
        for col in range(2):
            m = lambda k: self.modr[:, k * 8:(k + 1) * 8, col]
            rd = [self.modr, self.vtabB]
            self.stt(lt[:, 0, :, col], m(1), 1.0, g0, ALU.add, ALU.mult, rd, [lt])
            self.copy("dve", lt[:, 1, :, col], m(0), rd, [lt])
            self.tt(lt[:, 2, :, col], m(2), g1, ALU.mult, rd, [lt])
            self.stt(lt[:, 3, :, col], m(4), 1.0, g2, ALU.add, ALU.mult, rd, [lt])
            self.copy("dve", lt[:, 4, :, col], m(3), rd, [lt])
            self.tt(lt[:, 5, :, col], m(5), g3, ALU.mult, rd, [lt])

    def load_xT(self):
        S, A = self.S, self.A
        mk = A.mark()
        stg = [A.alloc("xstg%d" % i, [D], F32) for i in range(2)]
        for tb in range(T // 128):
            st = stg[tb % 2]
            S.dma("sp", st[:], self.dr["xs"][tb * 128:(tb + 1) * 128, :], writes=[st])
            for h in range(2):
                ps = self.ps[(tb * 2 + h) % 4]
                for j in range(4):
                    cch = h * 4 + j
                    S.op("pe", lambda e, j=j, cch=cch, ps=ps: e.transpose(ps[:, j * 128:(j + 1) * 128],
                                                                            st[:, cch * 128:(cch + 1) * 128], self.identf[:]),
                         reads=[st, self.identf], writes=[ps])
                self.copy("act" if h == 0 else "dve", self.xT[:, h * 4:(h + 1) * 4, tb * 128:(tb + 1) * 128],
                          ps[:, :].rearrange("p (j t) -> p j t", j=4), [ps], [self.xT])
        S.barrier()
        A.release(mk)

    def store_out(self):
        S, A = self.S, self.A
        mk = A.mark()
        stg = [A.alloc("ostg%d" % i, [D], F32) for i in range(2)]
        for tb in range(T // 128):
            st = stg[tb % 2]
            for h in range(2):
                ps = self.ps[(tb * 2 + h) % 4]
                for j in range(4):
                    cch = h * 4 + j
                    S.op("pe", lambda e, j=j, cch=cch, ps=ps: e.transpose(ps[:, j * 128:(j + 1) * 128],
                                                                            self.xT[:, cch, tb * 128:(tb + 1) * 128],
                                                                            self.identf[:]),
                         reads=[self.xT, self.identf], writes=[ps])
                self.copy("act" if h == 0 else "dve", st[:, h * 512:(h + 1) * 512], ps[:, :], [ps], [st])
            S.dma("sp", self.y_out[tb * 128:(tb + 1) * 128, :], st[:], reads=[st], writes=[self.Y], dbuf=st)
        S.barrier()
        A.release(mk)

    def rstd_from_ps(self, ps, n, nfeat):
        self.act(self.rt[:, 0:n], ps[:, 0:n], AF.Ln, [ps, self.epsb], [self.rt], bias=self.epsb[:, 0:1],
                 scale=1.0 / nfeat)
        self.act(self.rstd[:, 0:n], self.rt[:, 0:n], AF.Exp, [self.rt], [self.rstd], scale=-0.5)

    def stats(self, srcs, src_bufs, n, nfeat, dve_every=0):
        ps = self.ps[6]
        for k, (ap, b) in enumerate(zip(srcs, src_bufs)):
            sq = self.next_sq()
            if dve_every and k % dve_every == dve_every - 1:
                self.tt(sq[:, 0:n], ap, ap, ALU.mult, [b], [sq])
            else:
                self.act(sq[:, 0:n], ap, AF.Square, [b], [sq])
            self.mm(ps, ps[:, 0:n], self.onesb[:], sq[:, 0:n], k == 0, k == len(srcs) - 1, [self.onesb, sq])
        self.rstd_from_ps(ps, n, nfeat)

    def norm_mod(self, i, ka, t0, n, col, dst, doff):
        lt = self.lt[i]
        xT = self.xT
        self.stats([xT[:, c, t0:t0 + n] for c in range(NCH)], [xT] * NCH, n, D, dve_every=2)
        for c in range(NCH):
            tf = self.next_tf()
            self.stt(tf[:, 0:n], xT[:, c, t0:t0 + n], lt[:, ka, c, col:col + 1], self.rstd[:, 0:n], ALU.mult, ALU.mult,
                     [xT, lt, self.rstd], [tf])
            self.act(dst[:, c, doff:doff + n], tf[:, 0:n], AF.Identity, [tf, lt], [dst],
                     bias=lt[:, ka + 1, c, col:col + 1])

    def epilogue(self, i, kg, yb, yoff, t0, n, col):
        lt = self.lt[i]
        xT = self.xT
        self.stats([yb[:, c, yoff:yoff + n] for c in range(NCH)], [yb] * NCH, n, D)
        for c in range(NCH):
            tf = self.next_tf()
            self.stt(tf[:, 0:n], yb[:, c, yoff:yoff + n], lt[:, kg, c, col:col + 1], self.rstd[:, 0:n], ALU.mult,
                     ALU.mult, [yb, lt, self.rstd], [tf])
            self.tt(xT[:, c, t0:t0 + n], xT[:, c, t0:t0 + n], tf[:, 0:n], ALU.add, [xT, tf], [xT])

    def outproj(self, i, inT, wname, bias_name, in_bufs=None, pre_gen=None):
        S, A = self.S, self.A
        mk = A.mark()
        wres = A.alloc("wres", [8, D], BF16)
        ybs = [A.alloc("yb%d" % j, [8, 512], F32) for j in range(2)]
        wsrc = self.dr[wname].rearrange("(kc p) n -> p kc n", p=128)
        for h in range(2):
            S.dma("pool", wres[:, :, h * 512:(h + 1) * 512], wsrc[:, :, h * 512:(h + 1) * 512], writes=[wres])
        k = 0
        for ti, (t0, n, col) in enumerate(TILES):
            yb = ybs[ti % 2]
            inb = in_bufs[ti] if in_bufs is not None else inT
            pg_ = pre_gen(ti + 1) if (pre_gen is not None and ti + 1 < len(TILES)) else None
            for oc in range(NCH):
                ps = self.ps[k % 4]
                k += 1
                for kc in range(NCH):
                    self.mm(ps, ps[:, 0:n], wres[:, kc, oc * 128:(oc + 1) * 128], inT[:, kc, t0:t0 + n], kc == 0,
                            kc == NCH - 1, [wres, inb])
                self.drain(pg_, 2)
                if bias_name is not None:
                    self.act(yb[:, oc, 0:n], ps[:, 0:n], AF.Identity, [ps, self.vtabB], [yb],
                             bias=self.vcol(bias_name, oc, 1))
                else:
                    self.copy("act", yb[:, oc, 0:n], ps[:, 0:n], [ps], [yb])
            self.drain(pg_, 1000)
            self.epilogue(i, 2, yb, 0, t0, n, col)
        S.barrier()
        A.release(mk)

    @staticmethod
    def drain(gen, k=1):
        if gen is None:
            return
        for _ in range(k):
            try:
                next(gen)
            except StopIteration:
                return

    def norm_mod_gen(self, i, ka, t0, n, col, dst, doff):
        lt = self.lt[i]
        xT = self.xT
        self.stats([xT[:, c, t0:t0 + n] for c in range(NCH)], [xT] * NCH, n, D, dve_every=2)
        yield
        for c in range(NCH):
            tf = self.next_tf()
            self.stt(tf[:, 0:n], xT[:, c, t0:t0 + n], lt[:, ka, c, col:col + 1], self.rstd[:, 0:n], ALU.mult, ALU.mult,
                     [xT, lt, self.rstd], [tf])
            self.act(dst[:, c, doff:doff + n], tf[:, 0:n], AF.Identity, [tf, lt], [dst],
                     bias=lt[:, ka + 1, c, col:col + 1])
            yield

    def epilogue_gen(self, i, kg, yb, yoff, t0, n, col):
        lt = self.lt[i]
        xT = self.xT
        self.stats([yb[:, c, yoff:yoff + n] for c in range(NCH)], [yb] * NCH, n, D)
        yield
        for c in range(NCH):
            tf = self.next_tf()
            self.stt(tf[:, 0:n], yb[:, c, yoff:yoff + n], lt[:, kg, c, col:col + 1], self.rstd[:, 0:n], ALU.mult,
                     ALU.mult, [yb, lt, self.rstd], [tf])
            self.tt(xT[:, c, t0:t0 + n], xT[:, c, t0:t0 + n], tf[:, 0:n], ALU.add, [xT, tf], [xT])
            yield

    def chain(self, gens):
        for g in gens:
            yield from g

    def modulation_gen(self, i, ring, psi=7):
        S = self.S
        ps = self.ps[psi]
        wsrc = self.dr["w_mod"][i].rearrange("(kc p) n -> p kc n", p=128)
        for fc in range(48):
            slot, view = self.wload(ring, wsrc[:, :, fc * 128:(fc + 1) * 128], 8, 128)
            for kc in range(8):
                self.mm(ps, ps[:, fc * 2:fc * 2 + 2], view[:, kc, :], self.scT[:, kc, :], kc == 0, kc == 7,
                        [slot, self.scT])
            yield
        self.modulation_finish(i, ps)
        yield

    def ffn(self, i, embed_mod=None):
        S, A = self.S, self.A
        mk = A.mark()
        hTs = A.alloc("hTs", [NCH, 1280], BF16)
        hid = A.alloc("hid", [11, 1280], BF16)
        yb = [A.alloc("ybf", [NCH, 1280], F32)]
        ring = Ring(self, "fring", 3, 16 * 128)
        modg = None
        if embed_mod is not None:
            mring = Ring(self, "mring", 2, 8 * 128)
            modg = self.modulation_gen(embed_mod, mring)
        wg = self.dr["ffn_w_gate"][i].rearrange("(kc p) n -> p kc n", p=128)
        wu = self.dr["ffn_w_up"][i].rearrange("(kc p) n -> p kc n", p=128)
        wd = self.dr["ffn_w_down"][i].rearrange("(hc p) n -> p hc n", p=128)
        kk = 0
        pend_epi = None
        s0_, subs0 = FFN_ST[0]
        for (t0, n, col) in subs0:
            self.drain(self.norm_mod_gen(i, 3, t0, n, col, hTs, t0 - s0_), 100)
        for sti, (s0, subs) in enumerate(FFN_ST):
            nxt = FFN_ST[sti + 1] if sti + 1 < len(FFN_ST) else None
            pend_norm = None
            for half in range(2):
                for hl in range(11):
                    hc = half * 11 + hl
                    slot = ring.next()
                    view = slot.t[:, 0:16 * 128].rearrange("p (k c) -> p k c", k=16)
                    S.dma("pool", view[:, 0:8, :], wg[:, :, hc * 128:(hc + 1) * 128], writes=[slot])
                    S.dma("pool", view[:, 8:16, :], wu[:, :, hc * 128:(hc + 1) * 128], writes=[slot])
                    for (t0, n, col) in subs:
                        lo = t0 - s0
                        pg = self.ps[kk % 2]
                        pu = self.ps[2 + kk % 2]
                        kk += 1
                        for kc in range(NCH):
                            self.mm(pg, pg[:, 0:n], view[:, kc, :], hTs[:, kc, lo:lo + n], kc == 0, kc == NCH - 1,
                                    [slot, hTs])
                        for kc in range(NCH):
                            self.mm(pu, pu[:, 0:n], view[:, 8 + kc, :], hTs[:, kc, lo:lo + n], kc == 0, kc == NCH - 1,
                                    [slot, hTs])
                        tf = self.next_tf()
                        self.act(tf[:, 0:n], pg[:, 0:n], AF.Silu, [pg], [tf])
                        self.tt(hid[:, hl, lo:lo + n], tf[:, 0:n], pu[:, 0:n], ALU.mult, [tf, pu], [hid])
                        if pend_epi is not None:
                            self.drain(pend_epi, 1)
                    if modg is not None:
                        self.drain(modg, 1 if (sti, half) != (0, 0) else 2)
                if pend_epi is not None:
                    self.drain(pend_epi, 1000)
                    pend_epi = None
                if half == 1 and nxt is not None:
                    ns0, nsubs = nxt
                    pend_norm = self.chain([self.norm_mod_gen(i, 3, t0, n, col, hTs, t0 - ns0)
                                            for (t0, n, col) in nsubs])
                for oc in range(NCH):
                    slot = ring.next()
                    view = slot.t[:, 0:11 * 128].rearrange("p (k c) -> p k c", k=11)
                    S.dma("pool", view, wd[:, half * 11:half * 11 + 11, oc * 128:(oc + 1) * 128], writes=[slot])
                    for (t0, n, col) in subs:
                        lo = t0 - s0
                        ps = self.ps[4 + kk % 2]
                        kk += 1
                        for hl in range(11):
                            self.mm(ps, ps[:, 0:n], view[:, hl, :], hid[:, hl, lo:lo + n], hl == 0, hl == 10,
                                    [slot, hid])
                        if half == 0:
                            self.copy("act", yb[0][:, oc, lo:lo + n], ps[:, 0:n], [ps], [yb[0]])
                        else:
                            self.tt(yb[0][:, oc, lo:lo + n], yb[0][:, oc, lo:lo + n], ps[:, 0:n], ALU.add,
                                    [yb[0], ps], [yb[0]])
                        if pend_norm is not None:
                            self.drain(pend_norm, 2)
                if pend_norm is not None:
                    self.drain(pend_norm, 1000)
                    pend_norm = None
            if pend_epi is not None:
                self.drain(pend_epi, 1000)
            pend_epi = self.chain([self.epilogue_gen(i, 5, yb[0], t0 - s0, t0, n, col) for (t0, n, col) in subs])
            if nxt is None:
                self.drain(pend_epi, 1000)
                pend_epi = None
        if modg is not None:
            self.drain(modg, 1000)
        S.barrier()
        A.release(mk)
    @staticmethod
    def seq_pieces(t0, n):
        out = []
        for si, (s0, L, _) in enumerate(SEQS):
            lo = max(t0, s0)
            hi = min(t0 + n, s0 + L)
            if hi > lo:
                out.append((si, lo - s0, lo - t0, hi - lo))
        return out

    def conformer(self, i):
        S, A = self.S, self.A
        mk = A.mark()
        regB = A.alloc("regB", [NCH, T], BF16)
        mk2 = A.mark()
        PADW = 15
        upoff = []
        off = 0
        for (s0, L, _) in SEQS:
            upoff.append(off)
            off += L + 2 * PADW
        up = A.alloc("upad", [NCH, off], BF16)
        mk3 = A.mark()
        S.op("dve", lambda e: e.memset(up[:], 0.0), writes=[up])
        hT = regB
        hTb = [S.buf("hTt%d" % ti, regB.t) for ti in range(len(TILES))]
        ring = Ring(self, "cring", 3, 8 * 256)
        w = self.dr["cf_w_pw1"].rearrange("(kc p) n -> p kc n", p=128)
        kk = 0
        for c in range(NCH):
            slot = ring.next()
            view = slot.t[:, 0:8 * 256].rearrange("p (k c) -> p k c", k=8)
            S.dma("pool", view[:, :, 0:128], w[:, :, c * 128:(c + 1) * 128], writes=[slot])
            S.dma("pool", view[:, :, 128:256], w[:, :, D + c * 128:D + (c + 1) * 128], writes=[slot])
            for ti, (t0, n, col) in enumerate(TILES):
                if c == 0:
                    self.norm_mod(i, 0, t0, n, col, hTb[ti], t0)
                pa = self.ps[kk % 2]
                pg = self.ps[2 + kk % 2]
                kk += 1
                for kc in range(NCH):
                    self.mm(pa, pa[:, 0:n], view[:, kc, 0:128], hT[:, kc, t0:t0 + n], kc == 0, kc == NCH - 1,
                            [slot, hTb[ti]])
                for kc in range(NCH):
                    self.mm(pg, pg[:, 0:n], view[:, kc, 128:256], hT[:, kc, t0:t0 + n], kc == 0, kc == NCH - 1,
                            [slot, hTb[ti]])
                tf = self.next_tf()
                self.act(tf[:, 0:n], pg[:, 0:n], AF.Sigmoid, [pg, self.vtabB], [tf], bias=self.vcol("cf_b1", 8 + c, 1))
                for (si, pos, cl, ln) in self.seq_pieces(t0, n):
                    d0 = upoff[si] + PADW + pos
                    self.stt(up[:, c, d0:d0 + ln], pa[:, cl:cl + ln], self.vcol("cf_b1", c, 1), tf[:, cl:cl + ln],
                             ALU.add, ALU.mult, [pa, tf, self.vtabB], [up])
        S.barrier()
        A.release(mk3)
        cv = regB
        diag = [A.alloc("diag%d" % j, [31, 128], BF16) for j in range(2)]
        kk = 0
        for c in range(NCH):
            dg = diag[c % 2]
            for k in range(31):
                self.ts(dg[:, k, :], self.identb[:], self.vcol("cf_dw%d" % k, c, 1), None, ALU.mult, None,
                        [self.identb, self.vtabB], [dg])
            for si, (s0, L, _) in enumerate(SEQS):
                for q0 in range(0, L, 512):
                    n = min(512, L - q0)
                    ps = self.ps[kk % 4]
                    kk += 1
                    for k in range(31):
                        b0 = upoff[si] + q0 + k
                        self.mm(ps, ps[:, 0:n], dg[:, k, :], up[:, c, b0:b0 + n], k == 0, k == 30, [dg, up])
                    self.act(cv[:, c, s0 + q0:s0 + q0 + n], ps[:, 0:n], AF.Identity, [ps, self.vtabB], [cv],
                             bias=self.vcol("cf_dwb", c, 1))
        S.barrier()
        A.release(mk2)
        mean = A.alloc("lnmean", [512], F32)
        var = A.alloc("lnvar", [512], F32)
        lt1 = A.alloc("lnt1", [512], F32)
        lt2 = A.alloc("lnt2", [512], F32)
        pm = self.ps[5]
        pv = self.ps[7]
        cvb = [S.buf("cvt%d" % ti, regB.t) for ti in range(len(TILES))]

        def ln_gen(ti):
            (t0, n, col) = TILES[ti]
            cb = cvb[ti]
            for c in range(NCH):
                self.mm(pm, pm[:, 0:n], self.onesb[:], cv[:, c, t0:t0 + n], c == 0, c == NCH - 1, [self.onesb, cb])
            for c in range(NCH):
                sq = self.next_sq()
                self.act(sq[:, 0:n], cv[:, c, t0:t0 + n], AF.Square, [cb], [sq])
                self.mm(pv, pv[:, 0:n], self.onesb[:], sq[:, 0:n], c == 0, c == NCH - 1, [self.onesb, sq])
            yield
            self.act(mean[:, 0:n], pm[:, 0:n], AF.Copy, [pm], [mean], scale=1.0 / D)
            self.tt(lt1[:, 0:n], mean[:, 0:n], mean[:, 0:n], ALU.mult, [mean], [lt1])
            self.stt(var[:, 0:n], pv[:, 0:n], 1.0 / D, lt1[:, 0:n], ALU.mult, ALU.subtract, [pv, lt1], [var])
            self.act(self.rt[:, 0:n], var[:, 0:n], AF.Ln, [var, self.epsb], [self.rt], bias=self.epsb[:, 0:1])
            self.act(self.rstd[:, 0:n], self.rt[:, 0:n], AF.Exp, [self.rt], [self.rstd], scale=-0.5)
            self.copy("dve", var[:, 0:n], self.rstd[:, 0:n], [self.rstd], [var])
            yield
            for c in range(NCH):
                self.tt(lt1[:, 0:n], cv[:, c, t0:t0 + n], mean[:, 0:n], ALU.subtract, [cb, mean], [lt1])
                self.stt(lt2[:, 0:n], lt1[:, 0:n], self.vcol("cf_lng", c, 1), var[:, 0:n], ALU.mult, ALU.mult,
                         [lt1, var, self.vtabB], [lt2])
                self.act(cv[:, c, t0:t0 + n], lt2[:, 0:n], AF.Silu, [lt2, self.vtabB], [cb],
                         bias=self.vcol("cf_lnb", c, 1))
                yield

        self.drain(ln_gen(0), 1000)
        self.outproj(i, cv, "cf_w_pw2", "cf_b2", in_bufs=cvb, pre_gen=ln_gen)
        A.release(mk)

    def shortconv(self, i):
        S, A = self.S, self.A
        mk = A.mark()
        sT = A.alloc("sT", [NCH, T], BF16)
        mk2 = A.mark()
        hT = A.alloc("hT", [NCH, T], BF16)
        mpoff = [s0 + 2 * si for si, (s0, L, _) in enumerate(SEQS)]
        mp = A.alloc("mp", [T + 6], BF16)
        dgs = [A.alloc("scdg%d" % j, [3, 128], BF16) for j in range(2)]
        cvt = [A.alloc("cvt%d" % j, [512], F32) for j in range(2)]
        S.op("dve", lambda e: e.memset(mp[:], 0.0), writes=[mp])
        hTb = [S.buf("hTs%d" % ti, hT.t) for ti in range(len(TILES))]
        ring = Ring(self, "sring", 2, 8 * 384)
        w = self.dr["sc_w_in"].rearrange("(kc p) n -> p kc n", p=128)
        kk = 0
        for c in range(NCH):
            slot = ring.next()
            view = slot.t[:, 0:8 * 384].rearrange("p (k c) -> p k c", k=8)
            for g in range(3):
                S.dma("pool", view[:, :, g * 128:(g + 1) * 128], w[:, :, g * D + c * 128:g * D + (c + 1) * 128],
                      writes=[slot])
            dg = dgs[c % 2]
            for k in range(3):
                self.ts(dg[:, k, :], self.identb[:], self.vcol("sc_cw%d" % k, c, 1), None, ALU.mult, None,
                        [self.identb, self.vtabB], [dg])
            for ti, (t0, n, col) in enumerate(TILES):
                if c == 0:
                    self.norm_mod(i, 0, t0, n, col, hTb[ti], t0)
                pc = self.ps[kk % 2]
                ph = self.ps[2 + kk % 2]
                kk += 1
                for kc in range(NCH):
                    self.mm(pc, pc[:, 0:n], view[:, kc, 128:256], hT[:, kc, t0:t0 + n], kc == 0, kc == NCH - 1,
                            [slot, hTb[ti]])
                for kc in range(NCH):
                    self.mm(ph, ph[:, 0:n], view[:, kc, 256:384], hT[:, kc, t0:t0 + n], kc == 0, kc == NCH - 1,
                            [slot, hTb[ti]])
                tf = self.next_tf()
                self.copy("act", tf[:, 0:n], pc[:, 0:n], [pc], [tf])
                for (si, pos, cl, ln) in self.seq_pieces(t0, n):
                    d0 = mpoff[si] + 1 + pos
                    self.tt(mp[:, d0:d0 + ln], tf[:, cl:cl + ln], ph[:, cl:cl + ln], ALU.mult, [tf, ph], [mp])
            for ti, (t0, n, col) in enumerate(TILES):
                pb = self.ps[4 + kk % 2]
                pcv = self.ps[6 + kk % 2]
                kk += 1
                for kc in range(NCH):
                    self.mm(pb, pb[:, 0:n], view[:, kc, 0:128], hT[:, kc, t0:t0 + n], kc == 0, kc == NCH - 1,
                            [slot, hTb[ti]])
                for (si, pos, cl, ln) in self.seq_pieces(t0, n):
                    for k in range(3):
                        b0 = mpoff[si] + pos + k
                        self.mm(pcv, pcv[:, cl:cl + ln], dg[:, k, :], mp[:, b0:b0 + ln], k == 0, k == 2, [dg, mp])
                cv = cvt[kk % 2]
                self.copy("act", cv[:, 0:n], pcv[:, 0:n], [pcv], [cv])
                self.tt(sT[:, c, t0:t0 + n], cv[:, 0:n], pb[:, 0:n], ALU.mult, [cv, pb], [sT])
        S.barrier()
        A.release(mk2)
        self.outproj(i, sT, "sc_w_out", None)
        A.release(mk)

    def mla(self, i):
        S, A = self.S, self.A
        SCALE = 192.0 ** -0.5
        NK = T + PAST
        mk = A.mark()
        regB = A.alloc("regB", [NCH, T], BF16)
        mk1 = A.mark()
        cqn = A.alloc("cqn", [3, T], BF16)
        ckvT = A.alloc("ckvT", [2, NK], BF16)
        kpeT = A.alloc("kpeT", [NK], BF16)
        S.op("dve", lambda e: e.memset(kpeT[:], 0.0), writes=[kpeT])
        mk2 = A.mark()
        psb = [self.ps[j].t[:, :].bitcast(BF16) for j in range(8)]
        hT = regB
        hTb = [S.buf("hTm%d" % ti, regB.t) for ti in range(len(TILES))]
        ring1 = Ring(self, "m1ring", 2, 8 * 384)
        slot, view = self.wload(ring1, self.dr["mla_w_dq"].rearrange("(kc p) n -> p kc n", p=128), 8, 384)
        for ti, (t0, n, col) in enumerate(TILES):
            self.norm_mod(i, 0, t0, n, col, hTb[ti], t0)
        for ti, (t0, n, col) in enumerate(TILES):
            pq = [self.ps[(ti % 2) * 3 + c] for c in range(3)]
            for c in range(3):
                for kc in range(NCH):
                    self.mm(pq[c], pq[c][:, 0:n], view[:, kc, c * 128:(c + 1) * 128], hT[:, kc, t0:t0 + n], kc == 0,
                            kc == NCH - 1, [slot, hTb[ti]])
            self.stats([pq[c][:, 0:n] for c in range(3)], pq, n, 384)
            for c in range(3):
                self.stt(cqn[:, c, t0:t0 + n], pq[c][:, 0:n], self.vcol("gq", c, 1), self.rstd[:, 0:n], ALU.mult,
                         ALU.mult, [pq[c], self.rstd, self.vtabB], [cqn])
        if MLA_STAGE == 11:
            S.barrier()
            A.release(mk)
            return
        slotkv, viewkv = self.wload(ring1, self.dr["mla_w_dkv"].rearrange("(kc p) n -> p kc n", p=128), 8, 320)
        gkv = A.alloc("gkv", [256], F32)
        S.dma("sp", gkv[:], self.dr["mla_g_kv"][0:1, :].partition_broadcast(128), writes=[gkv])
        cosk = A.alloc("cosk", [16, 32], F32)
        sink = A.alloc("sink", [16, 32], F32)
        for tb in range(16):
            S.dma("sp", cosk[:, tb, :], self.dr["c_cosK"][tb * 128:(tb + 1) * 128, :], writes=[cosk])
            S.dma("sp", sink[:, tb, :], self.dr["c_sinK"][tb * 128:(tb + 1) * 128, :], writes=[sink])
        NB = 4
        kvst = [A.alloc("kvst%d" % j, [320], F32) for j in range(NB)]
        kvb = [A.alloc("kvb%d" % j, [320], BF16) for j in range(NB)]
        rps = [A.alloc("rp%d" % j, [4, 32], F32) for j in range(NB)]
        ssqs = [A.alloc("ssq%d" % j, [4], F32) for j in range(NB)]

        def cast_transpose(st, kb_, kc0, pidx):
            self.copy("dve", kb_[:, :], st[:, :], [st], [kb_])
            pb = self.ps[pidx]
            pv = psb[pidx]
            S.op("pe", lambda e: e.transpose(pv[:, 0:128], kb_[:, 0:128], self.identb[:]), reads=[kb_, self.identb],
                 writes=[pb])
            S.op("pe", lambda e: e.transpose(pv[:, 128:256], kb_[:, 128:256], self.identb[:]),
                 reads=[kb_, self.identb], writes=[pb])
            S.op("pe", lambda e: e.transpose(pv[0:64, 256:384], kb_[:, 256:320], self.identb[:]),
                 reads=[kb_, self.identb], writes=[pb])
            self.copy("dve", ckvT[:, :, kc0:kc0 + 128], pv[:, 0:256].rearrange("p (a b) -> p a b", a=2), [pb], [ckvT])
            self.copy("dve", kpeT[0:64, kc0:kc0 + 128], pv[0:64, 256:384], [pb], [kpeT])

        for b in range(2):
            st = kvst[b]
            S.dma("sp", st[:, 0:256], self.dr["cckv"][b * 128:(b + 1) * 128, :], writes=[st])
            S.dma("sp", st[:, 256:320], self.dr["ckpe"][b * 128:(b + 1) * 128, :], writes=[st])
            cast_transpose(st, kvb[b], b * 128, 4 + b)
        if MLA_STAGE == 12:
            S.barrier()
            A.release(mk)
            return
        for tb in range(T // 128):
            tok0 = tb * 128
            ps = self.ps[tb % NB]
            rp = rps[tb % NB]
            ssq = ssqs[tb % NB]
            for kc in range(NCH):
                self.mm(ps, ps[:, 0:320], hT[:, kc, tok0:tok0 + 128], viewkv[:, kc, :], kc == 0, kc == NCH - 1,
                        [slotkv, hTb[min(tok0 // 512, 4)]])
            tf = self.next_tf()
            self.act(tf[:, 0:256], ps[:, 0:256], AF.Square, [ps], [tf, ssq], accum_out=ssq[:, 0:1])
            self.act(ssq[:, 1:2], ssq[:, 0:1], AF.Sqrt, [ssq, self.epsb], [ssq], bias=self.epsb[:, 0:1],
                     scale=1.0 / 256)
            S.op("dve", lambda e, ssq=ssq: e.reciprocal(ssq[:, 2:3], ssq[:, 1:2]), reads=[ssq], writes=[ssq])
            st = kvst[tb % NB]
            self.stt(st[:, 0:256], ps[:, 0:256], ssq[:, 2:3], gkv[:], ALU.mult, ALU.mult, [ps, ssq, gkv], [st])
            if tok0 >= LS or MLA_STAGE == 13:
                self.copy("act", st[:, 256:320], ps[:, 256:320], [ps], [st])
                r0 = tok0 - LS
                S.dma("act", self.ckv_out[r0:r0 + 128, :], st[:, 0:256], reads=[st], writes=[self.CKVO], dbuf=st)
                S.dma("act", self.kpe_out[r0:r0 + 128, :], st[:, 256:320], reads=[st], writes=[self.CKVO], dbuf=st)
            else:
                x = ps[:, 256:320].rearrange("p (a s f) -> p a s f", a=2, s=2)
                o = st[:, 256:320].rearrange("p (a s f) -> p a s f", a=2, s=2)
                cs = cosk[:, tb, :].rearrange("p (a f) -> p a f", a=2)
                sn = sink[:, tb, :].rearrange("p (a f) -> p a f", a=2)
                r = [rp[:, j, :].rearrange("p (a f) -> p a f", a=2) for j in range(4)]
                self.tt(r[0], x[:, :, 0, :], cs, ALU.mult, [ps, cosk], [rp])
                self.tt(r[1], x[:, :, 1, :], sn, ALU.mult, [ps, sink], [rp])
                self.tt(r[2], x[:, :, 0, :], sn, ALU.mult, [ps, sink], [rp])
                self.tt(r[3], x[:, :, 1, :], cs, ALU.mult, [ps, cosk], [rp])
                self.tt(o[:, :, 0, :], r[0], r[1], ALU.subtract, [rp], [st])
                self.tt(o[:, :, 1, :], r[2], r[3], ALU.add, [rp], [st])
            if MLA_STAGE != 14:
                cast_transpose(st, kvb[tb % NB], tok0 + PAST, 4 + tb % NB)
        S.barrier()
        A.release(mk2)
        if MLA_STAGE in (1, 13, 14):
            A.release(mk)
            return
        oT = regB
        ring2 = Ring(self, "m2ring", 2, 1280)
        acc = A.alloc("dacc", [512], F32)
        accb = A.alloc("daccb", [512], BF16)
        knT = A.alloc("knT", [NK], BF16)
        V = A.alloc("V", [NK // 128, 128], BF16)
        qn = [A.alloc("qn%d" % j, [512], BF16) for j in range(2)]
        qp = [A.alloc("qp%d" % j, [512], BF16) for j in range(2)]
        Pt = [A.alloc("Pt%d" % j, [512], BF16) for j in range(3)]
        for b_ in qp:
            S.op("dve", lambda e, b_=b_: e.memset(b_[:], 0.0), writes=[b_])
        cosq = A.alloc("cosq", [512], F32)
        sinq = A.alloc("sinq", [512], F32)
        bnd = A.alloc("bnd", [16], F32)
        psq = [A.alloc("psq%d" % j, [512], BF16) for j in range(2)]
        ptf = [A.alloc("ptf%d" % j, [512], F32) for j in range(2)]
        wuq = self.dr["mla_w_uq"].rearrange("(kc p) n -> p kc n", p=128)
        wsw = self.dr["mla_w_uq_sw"].rearrange("(kc p) n -> p kc n", p=128)
        wkv = self.dr["mla_w_ukv"].rearrange("(kc p) n -> p kc n", p=128)
        groups = [(qt * 512, 512, list(range(0, 18)), True) for qt in range(4)]
        groups.append((LS, LP, [18, 19], False))
        groups.append((LS + LP, LP, [20, 21], False))
        gi = 0
        for h in range(8):
            slot = ring2.next()
            uq = slot.t[:, 0:576].rearrange("p (k c) -> p k c", k=3)
            sw = slot.t[:, 576:768].rearrange("p (k c) -> p k c", k=3)
            ukv = slot.t[:, 768:1280].rearrange("p (k c) -> p k c", k=2)
            S.dma("pool", uq, wuq[:, :, h * 192:(h + 1) * 192], writes=[slot])
            S.dma("pool", sw, wsw[:, :, h * 64:(h + 1) * 64], writes=[slot])
            S.dma("pool", ukv, wkv[:, :, h * 256:(h + 1) * 256], writes=[slot])
            first = True
            for kt in (range(0, NK, 512) if MLA_STAGE != 22 else []):
                n = min(512, NK - kt)
                ps = self.ps[(kt // 512) % 2]
                for kc in range(2):
                    self.mm(ps, ps[:, 0:n], ukv[:, kc, 0:128], ckvT[:, kc, kt:kt + n], kc == 0, kc == 1, [slot, ckvT])
                self.copy("dve", knT[:, kt:kt + n], ps[:, 0:n], [ps], [knT])
                sq = self.next_sq()
                self.act(sq[:, 0:n], ps[:, 0:n], AF.Square, [ps], [sq])
                sq2 = self.next_sq()
                self.act(sq2[0:64, 0:n], kpeT[0:64, kt:kt + n], AF.Square, [kpeT], [sq2])
                pb = self.ps[6]
                self.mm(pb, pb[:, 0:n], self.onesb[:], sq[:, 0:n], True, False, [self.onesb, sq])
                self.mm(pb, pb[:, 0:n], self.onesb[0:64, :], sq2[0:64, 0:n], False, True, [self.onesb, sq2])
                if first:
                    S.op("dve", lambda e, n=n, pb=pb: e.reduce_max(bnd[:, 0:1], pb[:, 0:n], AX.X), reads=[pb],
                         writes=[bnd])
                    first = False
                else:
                    S.op("dve", lambda e, n=n, pb=pb: e.reduce_max(bnd[:, 1:2], pb[:, 0:n], AX.X), reads=[pb],
                         writes=[bnd])
                    self.tt(bnd[:, 0:1], bnd[:, 0:1], bnd[:, 1:2], ALU.max, [bnd], [bnd])
            for kb0 in (range(0, NK // 128, 4) if MLA_STAGE != 21 else []):
                nb = min(4, NK // 128 - kb0)
                ps = self.ps[2 + (kb0 // 4) % 2]
                for j in range(nb):
                    kb = kb0 + j
                    for kc in range(2):
                        self.mm(ps, ps[:, j * 128:(j + 1) * 128], ckvT[:, kc, kb * 128:(kb + 1) * 128],
                                ukv[:, kc, 128:256], kc == 0, kc == 1, [slot, ckvT])
                self.copy("act", V[:, kb0:kb0 + nb, :], ps[:, 0:nb * 128].rearrange("p (j d) -> p j d", j=nb), [ps],
                          [V])
            def prep(g, slot=slot, uq=uq, sw=sw):
                (tok0, nq, kbs, is_s) = groups[g]
                par = g % 2
                qnb = qn[par]
                qpb = qp[par]
                b0 = 2 + 4 * par
                pn = self.ps[6]
                pp = self.ps[7]
                for kc in range(3):
                    self.mm(pn, pn[:, 0:nq], uq[:, kc, 0:128], cqn[:, kc, tok0:tok0 + nq], kc == 0, kc == 2,
                            [slot, cqn])
                for kc in range(3):
                    self.mm(pp, pp[0:64, 0:nq], uq[:, kc, 128:192], cqn[:, kc, tok0:tok0 + nq], kc == 0, kc == 2,
                            [slot, cqn])
                self.copy("dve", qnb[:, 0:nq], pn[:, 0:nq], [pn], [qnb])
                sq = psq[0]
                self.act(sq[:, 0:nq], pn[:, 0:nq], AF.Square, [pn], [sq])
                sq2 = psq[1]
                self.act(sq2[0:64, 0:nq], pp[0:64, 0:nq], AF.Square, [pp], [sq2])
                if is_s:
                    S.dma("sp", cosq[0:64, 0:nq], self.dr["c_cosQ"][:, tok0:tok0 + nq], writes=[cosq])
                    S.dma("sp", sinq[0:64, 0:nq], self.dr["c_sinQ"][:, tok0:tok0 + nq], writes=[sinq])
                    t1 = ptf[0]
                    self.tt(t1[0:64, 0:nq], pp[0:64, 0:nq], cosq[0:64, 0:nq], ALU.mult, [pp, cosq], [t1])
                    yield
                    for kc in range(3):
                        self.mm(pp, pp[0:64, 0:nq], sw[:, kc, :], cqn[:, kc, tok0:tok0 + nq], kc == 0, kc == 2,
                                [slot, cqn])
                    t2 = ptf[1]
                    self.tt(t2[0:64, 0:nq], pp[0:64, 0:nq], sinq[0:64, 0:nq], ALU.mult, [pp, sinq], [t2])
                    self.tt(qpb[0:64, 0:nq], t1[0:64, 0:nq], t2[0:64, 0:nq], ALU.add, [t1, t2], [qpb])
                else:
                    self.copy("dve", qpb[0:64, 0:nq], pp[0:64, 0:nq], [pp], [qpb])
                    yield
                pb = self.ps[6]
                self.mm(pb, pb[:, 0:nq], self.onesb[:], sq[:, 0:nq], True, False, [self.onesb, sq])
                self.mm(pb, pb[:, 0:nq], self.onesb[0:64, :], sq2[0:64, 0:nq], False, True, [self.onesb, sq2])
                S.op("dve", lambda e, nq=nq, pb=pb: e.reduce_max(bnd[:, b0:b0 + 1], pb[:, 0:nq], AX.X), reads=[pb],
                     writes=[bnd])
                self.tt(bnd[:, b0 + 1:b0 + 2], bnd[:, b0:b0 + 1], bnd[:, 0:1], ALU.mult, [bnd], [bnd])
                self.act(bnd[:, b0 + 2:b0 + 3], bnd[:, b0 + 1:b0 + 2], AF.Sqrt, [bnd], [bnd])
                self.ts(bnd[:, b0 + 3:b0 + 4], bnd[:, b0 + 2:b0 + 3], -SCALE, None, ALU.mult, None, [bnd], [bnd])
                yield

            ngr = len(groups) if MLA_STAGE not in (2, 21, 22) else 0
            if ngr:
                self.drain(prep(0), 10)
            for g in range(ngr):
                (tok0, nq, kbs, is_s) = groups[g]
                par = g % 2
                qnb = qn[par]
                qpb = qp[par]
                negM = bnd[:, 2 + 4 * par + 3:2 + 4 * par + 4]
                pg_ = prep(g + 1) if g + 1 < ngr else None
                po = self.ps[4]
                pd = self.ps[5]
                nkb = len(kbs)
                LA = 2
                for j in range(nkb + LA):
                    if j in (1, 8):
                        self.drain(pg_, 1)
                    if j < nkb:
                        kb = kbs[j]
                        pS = self.ps[j % 4]
                        self.mm(pS, pS[:, 0:nq], knT[:, kb * 128:(kb + 1) * 128], qnb[:, 0:nq], True, False,
                                [knT, qnb])
                        self.mm(pS, pS[:, 0:nq], kpeT[:, kb * 128:(kb + 1) * 128], qpb[:, 0:nq], False, True,
                                [kpeT, qpb])
                        P_ = Pt[j % 3]
                        self.act(P_[:, 0:nq], pS[:, 0:nq], AF.Exp, [pS, bnd], [P_], bias=negM, scale=SCALE)
                    if j >= LA:
                        kb = kbs[j - LA]
                        P_ = Pt[(j - LA) % 3]
                        self.mm(po, po[:, 0:nq], V[:, kb, :], P_[:, 0:nq], j == LA, j == nkb + LA - 1, [V, P_])
                        if j == LA:
                            self.copy("dve", acc[:, 0:nq], P_[:, 0:nq], [P_], [acc])
                        else:
                            self.tt(acc[:, 0:nq], acc[:, 0:nq], P_[:, 0:nq], ALU.add, [acc, P_], [acc])
                self.drain(pg_, 10)
                self.copy("dve", accb[:, 0:nq], acc[:, 0:nq], [acc], [accb])
                self.mm(pd, pd[:, 0:nq], self.onesb[:], accb[:, 0:nq], True, True, [self.onesb, accb])
                tf0 = self.next_tf()
                self.act(tf0[:, 0:nq], pd[:, 0:nq], AF.Ln, [pd], [tf0])
                tf = self.next_tf()
                self.act(tf[:, 0:nq], tf0[:, 0:nq], AF.Exp, [tf0], [tf], scale=-1.0)
                self.tt(oT[:, h, tok0:tok0 + nq], po[:, 0:nq], tf[:, 0:nq], ALU.mult, [po, tf], [oT])
        S.barrier()
        A.release(mk1)
        self.outproj(i, oT, "mla_w_o", None)
        A.release(mk)

    def alloc_at(self, name, off, shape, dt, parts=128):
        esz = 4 if dt == F32 else 2
        n = int(np.prod(shape)) * esz
        ap = self.A.t[0:parts, off:off + n].bitcast(dt)
        if len(shape) == 2:
            ap = ap.rearrange("p (a b) -> p a b", a=shape[0])
        elif len(shape) == 3:
            ap = ap.rearrange("p (a b c) -> p a b c", a=shape[0], b=shape[1])
        return self.S.buf(name, ap)

    def sin_act(self, dst, dstb, src_ps, nparts, n, scale_ap, bias_ap, rd):
        PI = float(np.pi)
        a, ta_, tb_ = self.sa
        self.act(a[0:nparts, 0:n], src_ps, AF.Identity, rd, [a], bias=bias_ap, scale=scale_ap)
        for _ in range(2):
            t1 = ta_
            self.ts(t1[0:nparts, 0:n], a[0:nparts, 0:n], PI, 2 * PI, ALU.is_gt, ALU.mult, [a], [t1])
            self.tt(a[0:nparts, 0:n], a[0:nparts, 0:n], t1[0:nparts, 0:n], ALU.subtract, [a, t1], [a])
            t2 = tb_
            self.ts(t2[0:nparts, 0:n], a[0:nparts, 0:n], -PI, 2 * PI, ALU.is_lt, ALU.mult, [a], [t2])
            self.tt(a[0:nparts, 0:n], a[0:nparts, 0:n], t2[0:nparts, 0:n], ALU.add, [a, t2], [a])
        self.act(dst, a[0:nparts, 0:n], AF.Sin, [a], [dstb])

    def hyena_filters(self):
        S, A = self.S, self.A
        mk = A.mark()
        xo = self.xT_off
        sumb = self.alloc_at("hsum", xo, [16, D], BF16)
        difb = self.alloc_at("hdif", xo + 32768, [16, D], BF16)
        w3 = self.alloc_at("hw3", xo + 65536, [4 * D], F32, parts=64)
        w1 = A.alloc("hw1", [64], F32)
        w2 = A.alloc("hw2", [64], F32)
        zT = A.alloc("hzT", [LS], F32)
        h1T = A.alloc("hh1T", [LS], F32)
        h2T = A.alloc("hh2T", [LS], F32)
        absd = A.alloc("habsd", [D], F32)
        dec = A.alloc("hdec", [D], F32)
        skipo = A.alloc("hskip", [D], F32)
        hst = [A.alloc("hst%d" % j, [2, D], F32) for j in range(2)]
        fring = [A.alloc("hfr%d" % j, [2, 16, 128], BF16) for j in range(2)]
        negt = A.alloc("hnegt", [16], F32)
        scl = A.alloc("hscl", [16], F32)
        nyq = A.alloc("hnyq", [16], BF16)
        fb = A.alloc("hfb", [2], F32)
        mring0 = Ring(self, "mring0", 2, 8 * 128)
        modg = self.modulation_gen(0, mring0, psi=5)
        self.sa = [A.alloc("hsa%d" % j, [512], F32) for j in range(3)]
        S.dma("sp", w3[:], self.dr["hy_f_w3"][:, :], writes=[w3])
        S.dma("sp", w1[0:33, :], self.dr["hy_f_w1"][:, :], writes=[w1])
        S.dma("sp", w2[0:64, :], self.dr["hy_f_w2"][:, :], writes=[w2])
        S.dma("sp", absd[:], self.dr["c_absdelta"][0:1, :].partition_broadcast(128), writes=[absd])
        fq0 = self.vcol("hy_fq0")
        fq1 = self.vcol("hy_fq1")
        self.tt(fb[0:64, 0:1], fq0[0:64, :], self.vcol("hy_fb1")[0:64, :], ALU.mult, [self.vtabB], [fb])
        self.tt(fb[0:64, 1:2], fq1[0:64, :], self.vcol("hy_fb2")[0:64, :], ALU.mult, [self.vtabB], [fb])
        kk = 0
        hsti = 0
        for L in (LS, LP):
            TC = L // 128
            FC = TC
            S.dma("sp", zT[0:33, 0:L], self.dr["c_zT%d" % L][:, :], writes=[zT])
            S.dma("sp", negt[:, 0:TC], self.dr["c_negt%d" % L][:, :], writes=[negt])
            S.dma("sp", scl[:, 0:FC], self.dr["c_dscl%d" % L][:, :], writes=[scl])
            S.dma("sp", nyq[:, 0:TC], self.dr["c_nyq%d" % L][:, :], writes=[nyq])
            for q0 in range(0, L, 512):
                n = min(512, L - q0)
                ps = self.ps[kk % 4]
                kk += 1
                self.mm(ps, ps[0:64, 0:n], w1[0:33, :], zT[0:33, q0:q0 + n], True, True, [w1, zT])
                self.sin_act(h1T[0:64, q0:q0 + n], h1T, ps[0:64, 0:n], 64, n, fq0[0:64, :], fb[0:64, 0:1],
                             [ps, self.vtabB, fb])
            for q0 in range(0, L, 512):
                n = min(512, L - q0)
                ps = self.ps[kk % 4]
                kk += 1
                self.mm(ps, ps[0:64, 0:n], w2[0:64, :], h1T[0:64, q0:q0 + n], True, True, [w2, h1T])
                self.sin_act(h2T[0:64, q0:q0 + n], h2T, ps[0:64, 0:n], 64, n, fq1[0:64, :], fb[0:64, 1:2],
                             [ps, self.vtabB, fb])
            for o in range(2):
                S.dma("sp", skipo[0:1, :], self.dr["hy_skip"][o:o + 1, :], writes=[skipo])
                for tb in range(TC):
                    self.act(dec[:], absd[:], AF.Exp, [absd, negt], [dec], scale=negt[:, tb:tb + 1])
                    for ch in range(2):
                        pf = self.ps[kk % 2]
                        pbk = self.ps[2 + kk % 2]
                        kk += 1
                        cf = o * D + ch * 512
                        cb = 2 * D + o * D + ch * 512
                        self.mm(pf, pf[:, :], h2T[0:64, tb * 128:(tb + 1) * 128], w3[0:64, cf:cf + 512], True, True,
                                [h2T, w3])
                        self.mm(pbk, pbk[:, :], h2T[0:64, tb * 128:(tb + 1) * 128], w3[0:64, cb:cb + 512], True, True,
                                [h2T, w3])
                        tb_ = self.next_tf()
                        self.copy("act", tb_[:, :], pbk[:, :], [pbk], [tb_])
                        s1 = self.next_tf()
                        self.tt(s1[:, :], pf[:, :], tb_[:, :], ALU.add, [pf, tb_], [s1])
                        if tb == 0:
                            self.tt(s1[0:1, :], s1[0:1, :], skipo[0:1, ch * 512:(ch + 1) * 512], ALU.add, [s1, skipo],
                                    [s1])
                        self.tt(sumb[:, tb, ch * 512:(ch + 1) * 512], s1[:, :], dec[:, ch * 512:(ch + 1) * 512],
                                ALU.mult, [s1, dec], [sumb])
                        d1 = self.next_tf()
                        self.tt(d1[:, :], pf[:, :], tb_[:, :], ALU.subtract, [pf, tb_], [d1])
                        self.tt(difb[:, tb, ch * 512:(ch + 1) * 512], d1[:, :], dec[:, ch * 512:(ch + 1) * 512],
                                ALU.mult, [d1, dec], [difb])
                        if tb == 0:
                            self.copy("dve", difb[0:1, 0, ch * 512:(ch + 1) * 512],
                                      sumb[0:1, 0, ch * 512:(ch + 1) * 512], [sumb], [difb])
                for fc in range(FC):
                    fr = fring[fc % 2]
                    S.dma("sp", fr[:, :, 0:TC, :], self.dr["c_dftF%d" % L][:, fc].rearrange("m p t f -> p m t f"),
                          writes=[fr])
                    hs = hst[hsti % 2]
                    hsti += 1
                    for ch in range(2):
                        pc = self.ps[kk % 2]
                        pn = self.ps[2 + kk % 2]
                        kk += 1
                        for tc in range(TC):
                            self.mm(pc, pc[:, :], fr[:, 0, tc, :], sumb[:, tc, ch * 512:(ch + 1) * 512], tc == 0,
                                    tc == TC - 1, [fr, sumb])
                        for tc in range(TC):
                            self.mm(pn, pn[:, :], fr[:, 1, tc, :], difb[:, tc, ch * 512:(ch + 1) * 512], tc == 0,
                                    tc == TC - 1, [fr, difb])
                        self.act(hs[:, 0, ch * 512:(ch + 1) * 512], pc[:, :], AF.Copy, [pc, scl], [hs],
                                 scale=scl[:, fc:fc + 1])
                        self.act(hs[:, 1, ch * 512:(ch + 1) * 512], pn[:, :], AF.Copy, [pn, scl], [hs],
                                 scale=scl[:, fc:fc + 1])
                        if fc == 0:
                            py = self.ps[7]
                            for tc in range(TC):
                                self.mm(py, py[0:1, :], nyq[:, tc:tc + 1], sumb[:, tc, ch * 512:(ch + 1) * 512],
                                        tc == 0, tc == TC - 1, [nyq, sumb])
                            self.act(hs[0:1, 1, ch * 512:(ch + 1) * 512], py[0:1, :], AF.Copy, [py, scl], [hs],
                                     scale=scl[0:1, 0:1])
                    S.dma("act", self.htab[L][o, :, fc].rearrange("m p c -> p m c"), hs[:, :, :], reads=[hs],
                          writes=[self.HT[L]], dbuf=hs)
                    self.drain(modg, 2)
        self.drain(modg, 1000)
        S.barrier()
        A.release(mk)

    def hyena_layer0(self):
        S, A = self.S, self.A
        self.hyena_filters()
        i = 0
        mk = A.mark()
        xo = self.xT_off
        vtok = self.alloc_at("vtok", xo, [T // 128, D], BF16)
        x1T = self.alloc_at("x1T", xo + 40960, [NCH, T], BF16)
        x2T = A.alloc("x2T", [NCH, T], BF16)
        mk2 = A.mark()
        hT = A.alloc("hT", [NCH, T], BF16)
        psb = [self.ps[j].t[:, :].bitcast(BF16) for j in range(8)]
        lt = self.lt[0]
        mk3 = A.mark()
        xst = [A.alloc("xst%d" % j, [D], F32) for j in range(2)]
        junks = [A.alloc("xjunk%d" % j, [D], F32) for j in range(2)]
        ssqs = [A.alloc("ssq0%d" % j, [4], F32) for j in range(2)]
        for tb in range(T // 128):
            col = 0 if tb * 128 < LS else 1
            st = xst[tb % 2]
            junk = junks[tb % 2]
            ssq = ssqs[tb % 2]
            S.dma("sp", st[:], self.dr["xs"][tb * 128:(tb + 1) * 128, :], writes=[st])
            self.act(junk[:], st[:], AF.Square, [st], [junk, ssq], accum_out=ssq[:, 0:1])
            self.act(ssq[:, 1:2], ssq[:, 0:1], AF.Sqrt, [ssq, self.epsb], [ssq], bias=self.epsb[:, 0:1], scale=1.0 / D)
            S.op("dve", lambda e, ssq=ssq: e.reciprocal(ssq[:, 2:3], ssq[:, 1:2]), reads=[ssq], writes=[ssq])
            self.ts(st[:], st[:], ssq[:, 2:3], None, ALU.mult, None, [st, ssq], [st])
            for h in range(2):
                ps = self.ps[(tb * 2 + h) % 4]
                for j in range(4):
                    cch = h * 4 + j
                    S.op("pe", lambda e, j=j, cch=cch, ps=ps, st=st: e.transpose(
                        ps[:, j * 128:(j + 1) * 128], st[:, cch * 128:(cch + 1) * 128], self.identf[:]),
                        reads=[st, self.identf], writes=[ps])
                for j in range(4):
                    cch = h * 4 + j
                    if h == 0:
                        self.act(hT[:, cch, tb * 128:(tb + 1) * 128], ps[:, j * 128:(j + 1) * 128], AF.Identity,
                                 [ps, lt], [hT], bias=lt[:, 1, cch, col:col + 1], scale=lt[:, 0, cch, col:col + 1])
                    else:
                        self.ts(hT[:, cch, tb * 128:(tb + 1) * 128], ps[:, j * 128:(j + 1) * 128],
                                lt[:, 0, cch, col:col + 1], lt[:, 1, cch, col:col + 1], ALU.mult, ALU.add, [ps, lt],
                                [hT])
        S.barrier()
        A.release(mk3)
        ppoff = [s0 + 2 * si for si, (s0, L, _) in enumerate(SEQS)]
        pres = [A.alloc("hpre%d" % j, [T + 6], BF16) for j in range(2)]
        vst = A.alloc("hvst", [T], BF16)
        for pre in pres:
            S.op("dve", lambda e, pre=pre: e.memset(pre[:], 0.0), writes=[pre])
        ring = Ring(self, "hring", 2, 8 * 256)
        w = self.dr["hy_w_in"].rearrange("(kc p) n -> p kc n", p=128)
        kk = 0
        for jj in range(12):
            slot, view = self.wload(ring, w[:, :, jj * 256:(jj + 1) * 256], 8, 256)
            for jl in range(2):
                j = jj * 2 + jl
                pre = pres[j % 2]
                for (t0, n, col) in TILES:
                    ps = self.ps[kk % 4]
                    kk += 1
                    for kc in range(NCH):
                        self.mm(ps, ps[:, 0:n], view[:, kc, jl * 128:(jl + 1) * 128], hT[:, kc, t0:t0 + n], kc == 0,
                                kc == NCH - 1, [slot, hT])
                    for (si, pos, cl, ln) in self.seq_pieces(t0, n):
                        d0 = ppoff[si] + 1 + pos
                        self.act(pre[:, d0:d0 + ln], ps[:, cl:cl + ln], AF.Identity, [ps, self.vtabB], [pre],
                                 bias=self.vcol("hy_b_in", j, 1))
                for (t0, n, col) in TILES:
                    ta = self.next_tf()
                    for (si, pos, cl, ln) in self.seq_pieces(t0, n):
                        p0 = ppoff[si] + pos
                        s0 = SEQS[si][0]
                        if j < 8:
                            dst = vst[:, s0 + pos:s0 + pos + ln]
                            db = vst
                        elif j < 16:
                            dst = x1T[:, j - 8, s0 + pos:s0 + pos + ln]
                            db = x1T
                        else:
                            dst = x2T[:, j - 16, s0 + pos:s0 + pos + ln]
                            db = x2T
                        self.ts(ta[:, cl:cl + ln], pre[:, p0:p0 + ln], self.vcol("hy_cw0", j, 1),
                                self.vcol("hy_cb", j, 1), ALU.mult, ALU.add, [pre, self.vtabB], [ta])
                        self.stt(ta[:, cl:cl + ln], pre[:, p0 + 1:p0 + 1 + ln], self.vcol("hy_cw1", j, 1),
                                 ta[:, cl:cl + ln], ALU.mult, ALU.add, [pre, ta, self.vtabB], [ta])
                        self.stt(dst, pre[:, p0 + 2:p0 + 2 + ln], self.vcol("hy_cw2", j, 1), ta[:, cl:cl + ln],
                                 ALU.mult, ALU.add, [pre, ta, self.vtabB], [db])
                if j < 8:
                    self.to_tokmajor(vst, None, vtok, j, psb)
        S.barrier()
        A.release(mk2)
        Y = A.alloc("hyY", [32, 512], BF16)
        hb = [A.alloc("hyH%d" % j, [2, 512], F32) for j in range(2)]
        fring = [A.alloc("hyfr%d" % j, [2, 16, 128], BF16) for j in range(2)]
        iring = [A.alloc("hyir%d" % j, [4, 512], BF16) for j in range(2)]
        cnt = {"f": 0, "i": 0, "h": 0, "k": 0}
        for o in range(2):
            gate = x1T if o == 0 else x2T
            for hf in range(2):
                for (s0, L, _) in SEQS:
                    TC = L // 128
                    FC = TC
                    tb0 = s0 // 128
                    for fc in range(FC):
                        fr = fring[cnt["f"] % 2]
                        cnt["f"] += 1
                        S.dma("sp", fr[:, :, 0:TC, :], self.dr["c_dftF%d" % L][:, fc].rearrange("m p t f -> p m t f"),
                              writes=[fr])
                        h_ = hb[cnt["h"] % 2]
                        cnt["h"] += 1
                        S.dma("sp", h_[:, :, :],
                              self.htab[L][o, :, fc, :, hf * 512:(hf + 1) * 512].rearrange("m p c -> p m c"),
                              reads=[self.HT[L]], writes=[h_])
                        pzc = self.ps[(cnt["k"] % 2) * 2]
                        pzs = self.ps[(cnt["k"] % 2) * 2 + 1]
                        cnt["k"] += 1
                        for tc in range(TC):
                            self.mm(pzc, pzc[:, :], fr[:, 0, tc, :], vtok[:, tb0 + tc, hf * 512:(hf + 1) * 512],
                                    tc == 0, tc == TC - 1, [fr, vtok])
                        for tc in range(TC):
                            self.mm(pzs, pzs[:, :], fr[:, 1, tc, :], vtok[:, tb0 + tc, hf * 512:(hf + 1) * 512],
                                    tc == 0, tc == TC - 1, [fr, vtok])
                        t1 = self.next_tf()
                        t2 = self.next_tf()
                        self.tt(t1[:, :], pzc[:, :], h_[:, 0, :], ALU.mult, [pzc, h_], [t1])
                        self.tt(t2[:, :], pzs[:, :], h_[:, 1, :], ALU.mult, [pzs, h_], [t2])
                        self.tt(Y[:, fc, :], t1[:, :], t2[:, :], ALU.subtract, [t1, t2], [Y])
                        t3 = self.next_tf()
                        self.tt(t3[:, :], pzc[:, :], h_[:, 1, :], ALU.mult, [pzc, h_], [t3])
                        t4 = self.next_tf()
                        self.tt(t4[:, :], pzs[:, :], h_[:, 0, :], ALU.mult, [pzs, h_], [t4])
                        self.tt(Y[:, FC + fc, :], t3[:, :], t4[:, :], ALU.add, [t3, t4], [Y])
                        if fc == 0:
                            self.tt(Y[0:1, 0, :], pzc[0:1, :], h_[0:1, 0, :], ALU.mult, [pzc, h_], [Y])
                            self.tt(Y[0:1, FC, :], pzs[0:1, :], h_[0:1, 1, :], ALU.mult, [pzs, h_], [Y])
                    for q0 in range(0, L, 512):
                        n = min(512, L - q0)
                        po = [self.ps[4 + cc] for cc in range(4)]
                        for kg in range(0, 2 * FC, 4):
                            ir = iring[cnt["i"] % 2]
                            cnt["i"] += 1
                            for kx in range(4):
                                k = kg + kx
                                m_, fcx = k // FC, k % FC
                                S.dma("sp", ir[:, kx, 0:n],
                                      self.dr["c_dftI%d" % L][m_, fcx * 128:(fcx + 1) * 128, q0:q0 + n], writes=[ir])
                            for kx in range(4):
                                k = kg + kx
                                for cc in range(4):
                                    self.mm(po[cc], po[cc][:, 0:n], Y[:, k, cc * 128:(cc + 1) * 128], ir[:, kx, 0:n],
                                            k == 0, k == 2 * FC - 1, [Y, ir])
                        for cc in range(4):
                            ch = hf * 4 + cc
                            g = gate[:, ch, s0 + q0:s0 + q0 + n]
                            self.tt(g, po[cc][:, 0:n], g, ALU.mult, [po[cc], gate], [gate])
            if o == 0:
                for j in range(NCH):
                    self.to_tokmajor(None, x1T, vtok, j, psb)
        S.barrier()
        A.release(mk2)
        self.load_xT()
        self.outproj(0, x2T, "hy_w_out", "hy_b_out")
        A.release(mk)

    def to_tokmajor(self, vst, srcT, vtok, j, psb):
        S = self.S
        for g in range(0, T // 128, 8):
            nb = min(8, T // 128 - g)
            pidx = 4 + (g // 8) % 2 if vst is not None else (g // 8) % 4
            pb = self.ps[pidx]
            pv = psb[pidx]
            for b in range(nb):
                tb = g + b
                if vst is not None:
                    src = vst[:, tb * 128:(tb + 1) * 128]
                    sb_ = vst
                else:
                    src = srcT[:, j, tb * 128:(tb + 1) * 128]
                    sb_ = srcT
                S.op("pe", lambda e, b=b, src=src, pv=pv: e.transpose(pv[:, b * 128:(b + 1) * 128], src, self.identb[:]),
                     reads=[sb_, self.identb], writes=[pb])
            self.copy("dve", vtok[:, g:g + nb, j * 128:(j + 1) * 128],
                      pv[:, 0:nb * 128].rearrange("p (b c) -> p b c", b=nb), [pb], [vtok])

_PROG_CACHE = {}


def _get_prog():
    key = (DEPTH_LIMIT, tuple(MIXERS))
    if key not in _PROG_CACHE:
        _PROG_CACHE[key] = Prog()
    return _PROG_CACHE[key]


def _make_in_maps(inputs):
    inp = {k: np.asarray(v) for k, v in inputs.items()}
    c = _consts()
    rows, _ = _vec_rows(inp)
    shared = {}
    for k in ["w_mod", "ffn_w_gate", "ffn_w_up", "ffn_w_down"]:
        shared[k] = np.ascontiguousarray(inp[k], dtype=np.float32)
    for k in ["hy_w_in", "hy_f_w1", "hy_f_w2", "hy_f_w3", "hy_skip", "hy_w_out", "cf_w_pw1", "cf_w_pw2", "sc_w_in",
              "sc_w_out", "mla_w_dq", "mla_w_uq", "mla_w_dkv", "mla_w_ukv", "mla_w_o"]:
        shared[k] = np.ascontiguousarray(inp[k][0], dtype=np.float32)
    shared["mla_g_kv"] = np.ascontiguousarray(inp["mla_g_kv"][0].reshape(1, 256), dtype=np.float32)
    wuq = shared["mla_w_uq"]
    sw = np.arange(64).reshape(2, 2, 16)[:, ::-1, :].reshape(64)
    shared["mla_w_uq_sw"] = np.ascontiguousarray(
        np.concatenate([wuq[:, h * 192 + 128 + sw] for h in range(8)], axis=1))
    for k, v in c.items():
        shared["c_" + k] = v
    in_maps = []
    for core in range(NCORES):
        m = dict(shared)
        xs = np.concatenate([inp["x_sample"][core], inp["x_prompt"][2 * core], inp["x_prompt"][2 * core + 1]], axis=0)
        m["xs"] = np.ascontiguousarray(xs, dtype=np.float32)
        cond = np.stack([inp["c"][core], inp["c_ctx"]], axis=0).astype(np.float32)
        crow = cond.reshape(2, 8, 128).transpose(1, 0, 2).reshape(16, 128)
        m["vecs"] = np.ascontiguousarray(np.concatenate([np.stack(rows, axis=0), crow], axis=0), dtype=np.float32)
        m["cckv"] = np.ascontiguousarray(inp["cache_ckv"][core, 0], dtype=np.float32)
        m["ckpe"] = np.ascontiguousarray(inp["cache_kpe"][core, 0], dtype=np.float32)
        in_maps.append(m)
    return in_maps


def kernel(**inputs):
    prog = _get_prog()
    in_maps = _make_in_maps(inputs)
    res = run_bass_kernel_spmd(prog.nc, in_maps, core_ids=list(range(NCORES)))
    y_prompt = np.zeros((16, LP, D), np.float32)
    y_sample = np.zeros((8, LS, D), np.float32)
    new_ckv = np.zeros((16, 1, LP, 256), np.float32)
    new_kpe = np.zeros((16, 1, LP, 64), np.float32)
    for core in range(NCORES):
        r = res.results[core]
        y = r["y"]
        y_sample[core] = y[0:LS]
        y_prompt[2 * core] = y[LS:LS + LP]
        y_prompt[2 * core + 1] = y[LS + LP:T]
        new_ckv[2 * core, 0] = r["nckv"][0:LP]
        new_ckv[2 * core + 1, 0] = r["nckv"][LP:2 * LP]
        new_kpe[2 * core, 0] = r["nkpe"][0:LP]
        new_kpe[2 * core + 1, 0] = r["nkpe"][LP:2 * LP]
    return (y_prompt, y_sample, new_ckv, new_kpe)
```

```python
import math
import numpy as np
import ml_dtypes
import concourse.bass as bass
import concourse.mybir as mybir
from concourse.bass_utils import run_bass_kernel_spmd

F32 = mybir.dt.float32
BF16 = mybir.dt.bfloat16
U8 = mybir.dt.uint8
AF = mybir.ActivationFunctionType
ALU = mybir.AluOpType
AX = mybir.AxisListType

D = 1024
NCH = 8
DFF = 2816
NHC = 22
LS = 2048
LP = 256
T = 2560
PAST = 256
EPS = 1e-6
NCORES = 8
ARENA_BYTES = 206 * 1024

DEPTH_LIMIT = 4
MIXERS = [0, 1, 2, 3]
DEBUG_X = False
MLA_STAGE = 0


class Buf:
    __slots__ = ("name", "w", "r", "dsem", "dcnt", "t", "excl")

    def __init__(self, name, t=None):
        self.name = name
        self.excl = False
        self.w = None
        self.r = {}
        self.dsem = None
        self.dcnt = 0
        self.t = t

    def __getitem__(self, k):
        return self.t[k]


class Sched:
    def __init__(self, nc):
        self.nc = nc
        self.engs = {"pe": nc.tensor, "act": nc.scalar, "dve": nc.vector, "pool": nc.gpsimd, "sp": nc.sync}
        self.sems = {}
        self.cnt = {}
        self.seen = {e: {} for e in self.engs}
        for e in self.engs:
            self.sems[e] = nc.alloc_semaphore(name="prog_" + e)
            self.cnt[e] = 0
        self.ndsem = 0
        self.bufs = []

    def buf(self, name, t=None):
        b = Buf(name, t)
        self.bufs.append(b)
        return b

    def _deps(self, reads, writes, skip=None, eng=None):
        deps = {}
        for b in reads:
            if b.w is not None:
                k, v = b.w
                if k != skip and deps.get(k, 0) < v:
                    deps[k] = v
            if b.excl:
                for k, v in b.r.items():
                    if k != eng and deps.get(k, 0) < v:
                        deps[k] = v
        for b in writes:
            if b.w is not None:
                k, v = b.w
                if k != skip and deps.get(k, 0) < v:
                    deps[k] = v
            for k, v in b.r.items():
                if k != skip and deps.get(k, 0) < v:
                    deps[k] = v
        return deps

    def _wait(self, eng, deps):
        e = self.engs[eng]
        seen = self.seen[eng]
        for k, v in deps.items():
            if seen.get(k, 0) >= v:
                continue
            e.wait_ge(self.sems[k], v)
            seen[k] = v

    def _record(self, ev, reads, writes):
        k, v = ev
        for b in writes:
            b.w = ev
            b.r = {}
        for b in reads:
            if b in writes:
                continue
            if b.r.get(k, 0) < v:
                b.r[k] = v

    def op(self, eng, fn, reads=(), writes=()):
        deps = self._deps(reads, writes, "pe" if eng == "pe" else None, eng)
        self._wait(eng, deps)
        inst = fn(self.engs[eng])
        self.cnt[eng] += 1
        inst.then_inc(self.sems[eng], 1)
        ev = (eng, self.cnt[eng])
        self._record(ev, reads, writes)
        return ev

    def dma(self, eng, out, in_, reads=(), writes=(), dbuf=None):
        if dbuf is None:
            dbuf = writes[0]
        if dbuf.dsem is None:
            key = "d%d" % self.ndsem
            self.ndsem += 1
            self.sems[key] = self.nc.alloc_semaphore(name=key + "_" + dbuf.name)
            dbuf.dsem = key
        deps = self._deps(reads, writes, skip=dbuf.dsem)
        self._wait(eng, deps)
        inst = self.engs[eng].dma_start(out=out, in_=in_)
        dbuf.dcnt += 16
        inst.then_inc(self.sems[dbuf.dsem], 16)
        ev = (dbuf.dsem, dbuf.dcnt)
        self._record(ev, reads, writes)
        return ev

    def _allev(self):
        allev = {}
        for e in self.engs:
            if self.cnt[e] > 0:
                allev[e] = self.cnt[e]
        for b in self.bufs:
            if b.dsem is not None and b.dcnt > 0:
                allev[b.dsem] = b.dcnt
        return allev

    def barrier(self):
        allev = self._allev()
        for e in self.engs:
            self._wait(e, {k: v for k, v in allev.items() if k != e})

    def finish(self, eng="sp"):
        allev = self._allev()
        self._wait(eng, {k: v for k, v in allev.items() if k != eng})


class Arena:
    def __init__(self, nc, S, nbytes):
        self.S = S
        self.n = nbytes
        self.t = nc.alloc_sbuf_tensor("arena", [128, nbytes], U8)
        self.off = 0
        self.peak = 0

    def alloc(self, name, shape, dt, parts=128):
        esz = 4 if dt == F32 else 2
        n = int(np.prod(shape)) * esz
        n_al = (n + 63) // 64 * 64
        if self.off + n_al > self.n:
            raise RuntimeError("SBUF arena overflow at %s: need %d have %d" % (name, n_al, self.n - self.off))
        ap = self.t[0:parts, self.off:self.off + n].bitcast(dt)
        if len(shape) == 2:
            ap = ap.rearrange("p (a b) -> p a b", a=shape[0])
        elif len(shape) == 3:
            ap = ap.rearrange("p (a b c) -> p a b c", a=shape[0], b=shape[1])
        self.off += n_al
        self.peak = max(self.peak, self.off)
        return self.S.buf(name, ap)

    def mark(self):
        return self.off

    def release(self, mark):
        self.off = mark


VEC_LAYOUT = None


def _vec_rows(inputs):
    rows = []
    idx = {}

    def add(name, v):
        v = np.asarray(v, np.float32).reshape(-1)
        n = v.shape[0]
        nr = (n + 127) // 128
        pad = np.zeros(nr * 128, np.float32)
        pad[:n] = v
        idx[name] = (len(rows), nr)
        for r in range(nr):
            rows.append(pad[r * 128:(r + 1) * 128])

    for i in range(4):
        for k in range(4):
            add("gn%d_%d" % (i, k), inputs["norm_g"][i, k])
        add("bmod%d" % i, inputs["b_mod"][i])
    add("hy_b_in", inputs["hy_b_in"][0])
    for k in range(3):
        add("hy_cw%d" % k, inputs["hy_conv_w"][0, k])
    add("hy_cb", inputs["hy_conv_b"][0])
    add("hy_b_out", inputs["hy_b_out"][0])
    add("hy_fb1", inputs["hy_f_b1"][0])
    add("hy_fb2", inputs["hy_f_b2"][0])
    add("hy_fq0", inputs["hy_f_freq"][0, 0])
    add("hy_fq1", inputs["hy_f_freq"][0, 1])
    add("cf_b1", inputs["cf_b_pw1"][0])
    for k in range(31):
        add("cf_dw%d" % k, inputs["cf_dw_w"][0, k])
    add("cf_dwb", inputs["cf_dw_b"][0])
    add("cf_lng", inputs["cf_ln_g"][0])
    add("cf_lnb", inputs["cf_ln_b"][0])
    add("cf_b2", inputs["cf_b_pw2"][0])
    for k in range(3):
        add("sc_cw%d" % k, inputs["sc_conv_w"][0, k])
    add("gq", inputs["mla_g_q"][0])
    return rows, idx


def _vec_index():
    dummy = {
        "norm_g": np.zeros((4, 4, D)), "b_mod": np.zeros((4, 6 * D)), "hy_b_in": np.zeros((1, 3 * D)),
        "hy_conv_w": np.zeros((1, 3, 3 * D)), "hy_conv_b": np.zeros((1, 3 * D)), "hy_b_out": np.zeros((1, D)),
        "hy_f_b1": np.zeros((1, 64)), "hy_f_b2": np.zeros((1, 64)), "hy_f_freq": np.zeros((1, 2, 64)),
        "cf_b_pw1": np.zeros((1, 2 * D)), "cf_dw_w": np.zeros((1, 31, D)), "cf_dw_b": np.zeros((1, D)),
        "cf_ln_g": np.zeros((1, D)), "cf_ln_b": np.zeros((1, D)), "cf_b_pw2": np.zeros((1, D)),
        "sc_conv_w": np.zeros((1, 3, D)), "mla_g_q": np.zeros((1, 384)),
    }
    rows, idx = _vec_rows(dummy)
    return len(rows), idx


def _dft_consts(L):
    N = 2 * L
    t = np.arange(L, dtype=np.float64)
    ang = 2.0 * np.pi * np.outer(t, t) / N
    C = np.cos(ang)
    Sm = np.sin(ang)
    nyq = (-1.0) ** t
    Sf = Sm.copy()
    Sf[:, 0] = nyq
    TC = L // 128
    FC = L // 128
    F = np.stack([C, Sf])
    F = F.reshape(2, TC, 128, FC, 128).transpose(0, 3, 2, 1, 4)
    Si = Sf.T
    I = np.stack([C, Si])
    scl = np.full((128, FC), 2.0 / N, np.float32)
    scl[0, 0] = 1.0 / N
    nyqcol = nyq.reshape(TC, 128).T.copy()
    return (np.ascontiguousarray(F).astype(ml_dtypes.bfloat16), np.ascontiguousarray(I).astype(ml_dtypes.bfloat16),
            scl, nyqcol.astype(ml_dtypes.bfloat16))


def _hyena_pos(L):
    f32 = np.float32
    t = np.linspace(0.0, 1.0, L, dtype=f32)[:, None]
    bands = 16
    w = (2.0 * np.pi * np.arange(L, dtype=f32)[:, None] / L).astype(f32)
    f = np.linspace(1e-4, bands - 1, bands, dtype=f32)[None, :]
    z = np.concatenate([t, np.cos(f * w), -np.sin(f * w)], axis=-1).astype(f32)
    negt = (-t[:, 0]).reshape(L // 128, 128).T.copy()
    return np.ascontiguousarray(z.T), negt.astype(f32)


def _rope_tables():
    f32 = np.float32
    rows = LS // 64
    row = np.repeat(np.arange(rows, dtype=f32), 64)
    col = np.tile(np.arange(64, dtype=f32), rows)
    nf = 16
    inv = np.exp(-math.log(10000.0) * np.arange(nf, dtype=f32) * (4.0 / 64)).astype(f32)
    ang = np.stack([row[:, None] * inv, col[:, None] * inv], axis=1)
    cos = np.cos(ang).astype(f32)
    sin = np.sin(ang).astype(f32)
    cosK = cos.reshape(LS, 32)
    sinK = sin.reshape(LS, 32)
    cosQ = np.zeros((64, LS), f32)
    sinQ = np.zeros((64, LS), f32)
    for a in range(2):
        for s in range(2):
            for f in range(16):
                d = a * 32 + s * 16 + f
                cosQ[d] = cos[:, a, f]
                sinQ[d] = sin[:, a, f] * (-1.0 if s == 0 else 1.0)
    return cosK, sinK, cosQ, sinQ


_CONST_CACHE = {}


def _consts():
    if _CONST_CACHE:
        return _CONST_CACHE
    c = {}
    for L in (LS, LP):
        Fm, Im, scl, nyqcol = _dft_consts(L)
        c["dftF%d" % L] = Fm
        c["dftI%d" % L] = Im
        c["dscl%d" % L] = scl
        c["nyq%d" % L] = nyqcol
        zT, negt = _hyena_pos(L)
        c["zT%d" % L] = zT
        c["negt%d" % L] = negt
    deltas = np.linspace(math.log(1e-2) / 1.5, math.log(1e-2) / 0.3, D, dtype=np.float32)
    c["absdelta"] = np.abs(deltas).reshape(1, D).astype(np.float32)
    cosK, sinK, cosQ, sinQ = _rope_tables()
    c["cosK"] = cosK
    c["sinK"] = sinK
    c["cosQ"] = cosQ
    c["sinQ"] = sinQ
    c["ident"] = np.eye(128, dtype=np.float32)
    _CONST_CACHE.update(c)
    return c


WEIGHT_NAMES = ["w_mod", "ffn_w_gate", "ffn_w_up", "ffn_w_down", "hy_w_in", "hy_f_w1", "hy_f_w2", "hy_f_w3",
                "hy_skip", "hy_w_out", "cf_w_pw1", "cf_w_pw2", "sc_w_in", "sc_w_out", "mla_w_dq", "mla_w_uq",
                "mla_w_dkv", "mla_g_kv", "mla_w_ukv", "mla_w_o"]


TILES = [(0, 512, 0), (512, 512, 0), (1024, 512, 0), (1536, 512, 0), (2048, 512, 1)]
SEQS = [(0, LS, True), (LS, LP, False), (LS + LP, LP, False)]
FFN_ST = [(0, [(0, 512, 0), (512, 512, 0), (1024, 256, 0)]),
          (1280, [(1280, 512, 0), (1792, 256, 0), (2048, 512, 1)])]


class Ring:
    def __init__(self, P, name, nslots, elems):
        self.P = P
        self.slots = [P.A.alloc("%s%d" % (name, i), [elems], BF16) for i in range(nslots)]
        self.i = 0

    def next(self):
        s = self.slots[self.i % len(self.slots)]
        self.i += 1
        return s


class Prog:
    def __init__(self):
        nc = bass.Bass("TRN2", target_bir_lowering=False)
        self.nc = nc
        self.S = Sched(nc)
        self.A = Arena(nc, self.S, ARENA_BYTES)
        self.NV, self.vidx = _vec_index()
        self.NVT = self.NV + 16
        self.dr = {}
        self.build()

    def din(self, name, shape, dt=F32):
        self.dr[name] = self.nc.dram_tensor(name, list(shape), dt, kind="ExternalInput").ap()
        return self.dr[name]

    def dout(self, name, shape):
        ap = self.nc.dram_tensor(name, list(shape), F32, kind="ExternalOutput").ap()
        self.dr[name] = ap
        return ap

    def vcol(self, name, j=0, n=1):
        r0, nr = self.vidx[name]
        return self.vtab[:, r0 + j:r0 + j + n]

    def mm(self, ps, out, lhsT, rhs, start, stop, reads):
        self.S.op("pe", lambda e: e.matmul(out, lhsT, rhs, start=start, stop=stop), reads=reads, writes=[ps])

    def act(self, out, in_, func, reads, writes, bias=None, scale=None, accum_out=None):
        kw = {}
        if bias is not None:
            kw["bias"] = bias
        if scale is not None:
            kw["scale"] = scale
        if accum_out is not None:
            kw["accum_out"] = accum_out
        self.S.op("act", lambda e: e.activation(out=out, in_=in_, func=func, **kw), reads=reads, writes=writes)

    def tt(self, out, in0, in1, op, reads, writes, eng="dve"):
        self.S.op(eng, lambda e: e.tensor_tensor(out, in0, in1, op), reads=reads, writes=writes)

    def ts(self, out, in0, s1, s2, op0, op1, reads, writes, eng="dve"):
        if op1 is None:
            self.S.op(eng, lambda e: e.tensor_scalar(out, in0, s1, None, op0), reads=reads, writes=writes)
        else:
            self.S.op(eng, lambda e: e.tensor_scalar(out, in0, s1, s2, op0, op1), reads=reads, writes=writes)

    def stt(self, out, in0, scalar, in1, op0, op1, reads, writes, eng="dve"):
        self.S.op(eng, lambda e: e.scalar_tensor_tensor(out=out, in0=in0, scalar=scalar, in1=in1, op0=op0, op1=op1),
                  reads=reads, writes=writes)

    def copy(self, eng, out, in_, reads, writes):
        if eng == "act":
            self.S.op("act", lambda e: e.copy(out, in_), reads=reads, writes=writes)
        else:
            self.S.op(eng, lambda e: e.tensor_copy(out, in_), reads=reads, writes=writes)

    def wload(self, ring, src, kc, ncols, eng="pool", koff=0, ktot=None):
        slot = ring.next()
        if ktot is None:
            ktot = kc
        view = slot.t[:, 0:ktot * ncols].rearrange("p (k c) -> p k c", k=ktot)
        self.S.dma(eng, view[:, koff:koff + kc, :], src, writes=[slot])
        return slot, view

    def build(self):
        nc, S, A = self.nc, self.S, self.A
        din = self.din
        c = _consts()
        din("xs", [T, D])
        din("vecs", [self.NVT, 128])
        din("cckv", [PAST, 256])
        din("ckpe", [PAST, 64])
        din("w_mod", [4, D, 6 * D])
        din("ffn_w_gate", [4, D, DFF])
        din("ffn_w_up", [4, D, DFF])
        din("ffn_w_down", [4, DFF, D])
        din("hy_w_in", [D, 3 * D])
        din("hy_f_w1", [33, 64])
        din("hy_f_w2", [64, 64])
        din("hy_f_w3", [64, 4 * D])
        din("hy_skip", [2, D])
        din("hy_w_out", [D, D])
        din("cf_w_pw1", [D, 2 * D])
        din("cf_w_pw2", [D, D])
        din("sc_w_in", [D, 3 * D])
        din("sc_w_out", [D, D])
        din("mla_w_dq", [D, 384])
        din("mla_w_uq", [384, 1536])
        din("mla_w_uq_sw", [384, 512])
        din("mla_w_dkv", [D, 320])
        din("mla_g_kv", [1, 256])
        din("mla_w_ukv", [256, 2048])
        din("mla_w_o", [D, D])
        for k, v in c.items():
            din("c_" + k, v.shape, BF16 if v.dtype == ml_dtypes.bfloat16 else F32)
        self.y_out = self.dout("y", [T, D])
        self.ckv_out = self.dout("nckv", [2 * LP, 256])
        self.kpe_out = self.dout("nkpe", [2 * LP, 64])
        self.Y = S.buf("Ydram")
        self.CKVO = S.buf("CKVOdram")
        self.htab = {}
        self.HT = {}
        for L in (LS, LP):
            self.htab[L] = nc.dram_tensor("htab%d" % L, [2, 2, L // 128, 128, D], F32, kind="Internal").ap()
            self.HT[L] = S.buf("HT%d" % L)

        self.ps = [S.buf("ps%d" % i, nc.alloc_psum_tensor("ps%d" % i, [128, 512], F32)) for i in range(8)]
        for b in self.ps:
            b.excl = True

        self.identf = A.alloc("identf", [128], F32)
        self.identb = A.alloc("identb", [128], BF16)
        self.onesb = A.alloc("onesb", [128], BF16)
        self.vtab = None
        vt = A.alloc("vtab", [self.NVT], F32)
        self.vtabB = vt
        self.vtab = vt.t
        self.scT = A.alloc("scT", [8, 2], BF16)
        self.lt = [A.alloc("lt%d" % i, [6, 8, 2], F32) for i in range(4)]
        self.modr = A.alloc("modr", [48, 2], F32)
        self.sq = [A.alloc("sq%d" % i, [512], BF16) for i in range(2)]
        self.rt = A.alloc("rt", [512], F32)
        self.rstd = A.alloc("rstd", [512], F32)
        self.tf = [A.alloc("tf%d" % i, [512], F32) for i in range(3)]
        self.tfi = 0
        self.sqi = 0
        self.epsb = A.alloc("epsb", [1], F32)
        self.xT_off = A.mark()
        self.xT = A.alloc("xT", [NCH, T], F32)
        self.base = A.mark()

        S.dma("sp", self.identf[:], self.dr["c_ident"][:, :], writes=[self.identf])
        self.copy("act", self.identb[:], self.identf[:], [self.identf], [self.identb])
        S.op("dve", lambda e: e.memset(self.onesb[:], 1.0), writes=[self.onesb])
        S.op("dve", lambda e: e.memset(self.epsb[:], EPS), writes=[self.epsb])
        self.load_vecs()
        depth = DEPTH_LIMIT
        if not (MIXERS[0] == 0):
            self.modulation(0)
        for i in range(depth):
            m = MIXERS[i]
            if m == 0 and i == 0:
                self.hyena_layer0()
            else:
                if i == 0:
                    self.load_xT()
                if m == 0:
                    raise NotImplementedError
                elif m == 1:
                    self.conformer(i)
                elif m == 2:
                    self.shortconv(i)
                else:
                    self.mla(i)
            self.ffn(i, embed_mod=(i + 1 if i + 1 < depth else None))
        self.store_out()
        S.finish("sp")

    def next_tf(self):
        b = self.tf[self.tfi % 3]
        self.tfi += 1
        return b

    def next_sq(self):
        b = self.sq[self.sqi % 2]
        self.sqi += 1
        return b

    def load_vecs(self):
        S, A = self.S, self.A
        mk = A.mark()
        stg = A.alloc("vstg", [128], F32)
        ps = self.ps[7]
        for g in range((self.NVT + 127) // 128):
            r0 = g * 128
            nr = min(128, self.NVT - r0)
            S.dma("sp", stg[0:nr, :], self.dr["vecs"][r0:r0 + nr, :], writes=[stg])
            S.op("pe", lambda e: e.transpose(ps[:, 0:nr], stg[0:nr, :], self.identf[0:nr, 0:nr]),
                 reads=[stg, self.identf], writes=[ps])
            self.copy("dve", self.vtab[:, r0:r0 + nr], ps[:, 0:nr], [ps], [self.vtabB])
        cc = self.vtab[:, self.NV:self.NV + 16]
        self.act(self.scT.t.rearrange("p a b -> p (a b)"), cc, AF.Silu, [self.vtabB], [self.scT])
        S.barrier()
        A.release(mk)

    def modulation(self, i):
        S, A = self.S, self.A
        mk = A.mark()
        ring = Ring(self, "mring", 3, 8 * 384)
        ps = self.ps[7]
        wsrc = self.dr["w_mod"][i].rearrange("(kc p) n -> p kc n", p=128)
        for g in range(16):
            slot, view = self.wload(ring, wsrc[:, :, g * 384:(g + 1) * 384], 8, 384)
            for f in range(3):
                fc = g * 3 + f
                for kc in range(8):
                    self.mm(ps, ps[:, fc * 2:fc * 2 + 2], view[:, kc, f * 128:(f + 1) * 128], self.scT[:, kc, :],
                            kc == 0, kc == 7, [slot, self.scT])
        self.modulation_finish(i, ps)
        S.barrier()
        A.release(mk)

    def modulation_finish(self, i, ps):
        r0, _ = self.vidx["bmod%d" % i]
        bm = self.vtab[:, r0:r0 + 48]
        psv = ps[:, 0:96].rearrange("p (f c) -> p f c", c=2)
        for col in range(2):
            self.tt(self.modr[:, :, col], psv[:, :, col], bm, ALU.add, [ps, self.vtabB], [self.modr])
        lt = self.lt[i]
        g0 = self.vcol("gn%d_0" % i, 0, 8)
        g1 = self.vcol("gn%d_1" % i, 0, 8)
        g2 = self.vcol("gn%d_2" % i, 0, 8)
        g3 = self.vcol("gn%d_3" % i, 0, 8)
        for col in range(2):
            m = lambda k: self.modr[:, k * 8:(k + 1) * 8, col]
            rd = [self.modr, self.vtabB]
            self.stt(lt[:, 0, :, col], m(1), 1.0, g0, ALU.add, ALU.mult, rd, [lt])
            self.copy("dve", lt[:, 1, :, col], m(0), rd, [lt])
            self.tt(lt[:, 2, :, col], m(2), g1, ALU.mult, rd, [lt])
            self.stt(lt[:, 3, :, col], m(4), 1.0, g2, ALU.add, ALU.mult, rd, [lt])
            self.copy("dve", lt[:, 4, :, col], m(3), rd, [lt])
            self.tt(lt[:, 5, :, col], m(5), g3, ALU.mult, rd, [lt])

    def load_xT(self):
        S, A = self.S, self.A
        mk = A.mark()
        stg = [A.alloc("xstg%d" % i, [D], F32) for i in range(2)]
        for tb in range(T // 128):
            st = stg[tb % 2]
            S.dma("sp", st[:], self.dr["xs"][tb * 128:(tb + 1) * 128, :], writes=[st])
            for h in range(2):
                ps = self.ps[(tb * 2 + h) % 4]
                for j in range(4):
                    cch = h * 4 + j
                    S.op("pe", lambda e, j=j, cch=cch, ps=ps: e.transpose(ps[:, j * 128:(j + 1) * 128],
                                                                            st[:, cch * 128:(cch + 1) * 128], self.identf[:]),
                         reads=[st, self.identf], writes=[ps])
                self.copy("act" if h == 0 else "dve", self.xT[:, h * 4:(h + 1) * 4, tb * 128:(tb + 1) * 128],
                          ps[:, :].rearrange("p (j t) -> p j t", j=4), [ps], [self.xT])
        S.barrier()
        A.release(mk)

    def store_out(self):
        S, A = self.S, self.A
        mk = A.mark()
        stg = [A.alloc("ostg%d" % i, [D], F32) for i in range(2)]
        for tb in range(T // 128):
            st = stg[tb % 2]
            for h in range(2):
                ps = self.ps[(tb * 2 + h) % 4]
                for j in range(4):
                    cch = h * 4 + j
                    S.op("pe", lambda e, j=j, cch=cch, ps=ps: e.transpose(ps[:, j * 128:(j + 1) * 128],
                                                                            self.xT[:, cch, tb * 128:(tb + 1) * 128],
                                                                            self.identf[:]),
                         reads=[self.xT, self.identf], writes=[ps])
                self.copy("act" if h == 0 else "dve", st[:, h * 512:(h + 1) * 512], ps[:, :], [ps], [st])
            S.dma("sp", self.y_out[tb * 128:(tb + 1) * 128, :], st[:], reads=[st], writes=[self.Y], dbuf=st)
        S.barrier()
        A.release(mk)

    def rstd_from_ps(self, ps, n, nfeat):
        self.act(self.rt[:, 0:n], ps[:, 0:n], AF.Ln, [ps, self.epsb], [self.rt], bias=self.epsb[:, 0:1],
                 scale=1.0 / nfeat)
        self.act(self.rstd[:, 0:n], self.rt[:, 0:n], AF.Exp, [self.rt], [self.rstd], scale=-0.5)

    def stats(self, srcs, src_bufs, n, nfeat, dve_every=0):
        ps = self.ps[6]
        for k, (ap, b) in enumerate(zip(srcs, src_bufs)):
            sq = self.next_sq()
            if dve_every and k % dve_every == dve_every - 1:
                self.tt(sq[:, 0:n], ap, ap, ALU.mult, [b], [sq])
            else:
                self.act(sq[:, 0:n], ap, AF.Square, [b], [sq])
            self.mm(ps, ps[:, 0:n], self.onesb[:], sq[:, 0:n], k == 0, k == len(srcs) - 1, [self.onesb, sq])
        self.rstd_from_ps(ps, n, nfeat)

    def norm_mod(self, i, ka, t0, n, col, dst, doff):
        lt = self.lt[i]
        xT = self.xT
        self.stats([xT[:, c, t0:t0 + n] for c in range(NCH)], [xT] * NCH, n, D, dve_every=2)
        for c in range(NCH):
            tf = self.next_tf()
            self.stt(tf[:, 0:n], xT[:, c, t0:t0 + n], lt[:, ka, c, col:col + 1], self.rstd[:, 0:n], ALU.mult, ALU.mult,
                     [xT, lt, self.rstd], [tf])
            self.act(dst[:, c, doff:doff + n], tf[:, 0:n], AF.Identity, [tf, lt], [dst],
                     bias=lt[:, ka + 1, c, col:col + 1])

    def epilogue(self, i, kg, yb, yoff, t0, n, col):
        lt = self.lt[i]
        xT = self.xT
        self.stats([yb[:, c, yoff:yoff + n] for c in range(NCH)], [yb] * NCH, n, D)
        for c in range(NCH):
            tf = self.next_tf()
            self.stt(tf[:, 0:n], yb[:, c, yoff:yoff + n], lt[:, kg, c, col:col + 1], self.rstd[:, 0:n], ALU.mult,
                     ALU.mult, [yb, lt, self.rstd], [tf])
            self.tt(xT[:, c, t0:t0 + n], xT[:, c, t0:t0 + n], tf[:, 0:n], ALU.add, [xT, tf], [xT])

    def outproj(self, i, inT, wname, bias_name, in_bufs=None, pre_gen=None):
        S, A = self.S, self.A
        mk = A.mark()
        wres = A.alloc("wres", [8, D], BF16)
        ybs = [A.alloc("yb%d" % j, [8, 512], F32) for j in range(2)]
        wsrc = self.dr[wname].rearrange("(kc p) n -> p kc n", p=128)
        for h in range(2):
            S.dma("pool", wres[:, :, h * 512:(h + 1) * 512], wsrc[:, :, h * 512:(h + 1) * 512], writes=[wres])
        k = 0
        for ti, (t0, n, col) in enumerate(TILES):
            yb = ybs[ti % 2]
            inb = in_bufs[ti] if in_bufs is not None else inT
            pg_ = pre_gen(ti + 1) if (pre_gen is not None and ti + 1 < len(TILES)) else None
            for oc in range(NCH):
                ps = self.ps[k % 4]
                k += 1
                for kc in range(NCH):
                    self.mm(ps, ps[:, 0:n], wres[:, kc, oc * 128:(oc + 1) * 128], inT[:, kc, t0:t0 + n], kc == 0,
                            kc == NCH - 1, [wres, inb])
                self.drain(pg_, 2)
                if bias_name is not None:
                    self.act(yb[:, oc, 0:n], ps[:, 0:n], AF.Identity, [ps, self.vtabB], [yb],
                             bias=self.vcol(bias_name, oc, 1))
                else:
                    self.copy("act", yb[:, oc, 0:n], ps[:, 0:n], [ps], [yb])
            self.drain(pg_, 1000)
            self.epilogue(i, 2, yb, 0, t0, n, col)
        S.barrier()
        A.release(mk)

    @staticmethod
    def drain(gen, k=1):
        if gen is None:
            return
        for _ in range(k):
            try:
                next(gen)
            except StopIteration:
                return

    def norm_mod_gen(self, i, ka, t0, n, col, dst, doff):
        lt = self.lt[i]
        xT = self.xT
        self.stats([xT[:, c, t0:t0 + n] for c in range(NCH)], [xT] * NCH, n, D, dve_every=2)
        yield
        for c in range(NCH):
            tf = self.next_tf()
            self.stt(tf[:, 0:n], xT[:, c, t0:t0 + n], lt[:, ka, c, col:col + 1], self.rstd[:, 0:n], ALU.mult, ALU.mult,
                     [xT, lt, self.rstd], [tf])
            self.act(dst[:, c, doff:doff + n], tf[:, 0:n], AF.Identity, [tf, lt], [dst],
                     bias=lt[:, ka + 1, c, col:col + 1])
            yield

    def epilogue_gen(self, i, kg, yb, yoff, t0, n, col):
        lt = self.lt[i]
        xT = self.xT
        self.stats([yb[:, c, yoff:yoff + n] for c in range(NCH)], [yb] * NCH, n, D)
        yield
        for c in range(NCH):
            tf = self.next_tf()
            self.stt(tf[:, 0:n], yb[:, c, yoff:yoff + n], lt[:, kg, c, col:col + 1], self.rstd[:, 0:n], ALU.mult,
                     ALU.mult, [yb, lt, self.rstd], [tf])
            self.tt(xT[:, c, t0:t0 + n], xT[:, c, t0:t0 + n], tf[:, 0:n], ALU.add, [xT, tf], [xT])
            yield

    def chain(self, gens):
        for g in gens:
            yield from g

    def modulation_gen(self, i, ring, psi=7):
        S = self.S
        ps = self.ps[psi]
        wsrc = self.dr["w_mod"][i].rearrange("(kc p) n -> p kc n", p=128)
        for fc in range(48):
            slot, view = self.wload(ring, wsrc[:, :, fc * 128:(fc + 1) * 128], 8, 128)
            for kc in range(8):
                self.mm(ps, ps[:, fc * 2:fc * 2 + 2], view[:, kc, :], self.scT[:, kc, :], kc == 0, kc == 7,
                        [slot, self.scT])
            yield
        self.modulation_finish(i, ps)
        yield

    def ffn(self, i, embed_mod=None):
        S, A = self.S, self.A
        mk = A.mark()
        hTs = A.alloc("hTs", [NCH, 1280], BF16)
        hid = A.alloc("hid", [11, 1280], BF16)
        yb = [A.alloc("ybf", [NCH, 1280], F32)]
        ring = Ring(self, "fring", 3, 16 * 128)
        modg = None
        if embed_mod is not None:
            mring = Ring(self, "mring", 2, 8 * 128)
            modg = self.modulation_gen(embed_mod, mring)
        wg = self.dr["ffn_w_gate"][i].rearrange("(kc p) n -> p kc n", p=128)
        wu = self.dr["ffn_w_up"][i].rearrange("(kc p) n -> p kc n", p=128)
        wd = self.dr["ffn_w_down"][i].rearrange("(hc p) n -> p hc n", p=128)
        kk = 0
        pend_epi = None
        s0_, subs0 = FFN_ST[0]
        for (t0, n, col) in subs0:
            self.drain(self.norm_mod_gen(i, 3, t0, n, col, hTs, t0 - s0_), 100)
        for sti, (s0, subs) in enumerate(FFN_ST):
            nxt = FFN_ST[sti + 1] if sti + 1 < len(FFN_ST) else None
            pend_norm = None
            for half in range(2):
                for hl in range(11):
                    hc = half * 11 + hl
                    slot = ring.next()
                    view = slot.t[:, 0:16 * 128].rearrange("p (k c) -> p k c", k=16)
                    S.dma("pool", view[:, 0:8, :], wg[:, :, hc * 128:(hc + 1) * 128], writes=[slot])
                    S.dma("pool", view[:, 8:16, :], wu[:, :, hc * 128:(hc + 1) * 128], writes=[slot])
                    for (t0, n, col) in subs:
                        lo = t0 - s0
                        pg = self.ps[kk % 2]
                        pu = self.ps[2 + kk % 2]
                        kk += 1
                        for kc in range(NCH):
                            self.mm(pg, pg[:, 0:n], view[:, kc, :], hTs[:, kc, lo:lo + n], kc == 0, kc == NCH - 1,
                                    [slot, hTs])
                        for kc in range(NCH):
                            self.mm(pu, pu[:, 0:n], view[:, 8 + kc, :], hTs[:, kc, lo:lo + n], kc == 0, kc == NCH - 1,
                                    [slot, hTs])
                        tf = self.next_tf()
                        self.act(tf[:, 0:n], pg[:, 0:n], AF.Silu, [pg], [tf])
                        self.tt(hid[:, hl, lo:lo + n], tf[:, 0:n], pu[:, 0:n], ALU.mult, [tf, pu], [hid])
                        if pend_epi is not None:
                            self.drain(pend_epi, 1)
                    if modg is not None:
                        self.drain(modg, 1 if (sti, half) != (0, 0) else 2)
                if pend_epi is not None:
                    self.drain(pend_epi, 1000)
                    pend_epi = None
                if half == 1 and nxt is not None:
                    ns0, nsubs = nxt
                    pend_norm = self.chain([self.norm_mod_gen(i, 3, t0, n, col, hTs, t0 - ns0)
                                            for (t0, n, col) in nsubs])
                for oc in range(NCH):
                    slot = ring.next()
                    view = slot.t[:, 0:11 * 128].rearrange("p (k c) -> p k c", k=11)
                    S.dma("pool", view, wd[:, half * 11:half * 11 + 11, oc * 128:(oc + 1) * 128], writes=[slot])
                    for (t0, n, col) in subs:
                        lo = t0 - s0
                        ps = self.ps[4 + kk % 2]
                        kk += 1
                        for hl in range(11):
                            self.mm(ps, ps[:, 0:n], view[:, hl, :], hid[:, hl, lo:lo + n], hl == 0, hl == 10,
                                    [slot, hid])
                        if half == 0:
                            self.copy("act", yb[0][:, oc, lo:lo + n], ps[:, 0:n], [ps], [yb[0]])
                        else:
                            self.tt(yb[0][:, oc, lo:lo + n], yb[0][:, oc, lo:lo + n], ps[:, 0:n], ALU.add,
                                    [yb[0], ps], [yb[0]])
                        if pend_norm is not None:
                            self.drain(pend_norm, 2)
                if pend_norm is not None:
                    self.drain(pend_norm, 1000)
                    pend_norm = None
            if pend_epi is not None:
                self.drain(pend_epi, 1000)
            pend_epi = self.chain([self.epilogue_gen(i, 5, yb[0], t0 - s0, t0, n, col) for (t0, n, col) in subs])
            if nxt is None:
                self.drain(pend_epi, 1000)
                pend_epi = None
        if modg is not None:
            self.drain(modg, 1000)
        S.barrier()
        A.release(mk)
    @staticmethod
    def seq_pieces(t0, n):
        out = []
        for si, (s0, L, _) in enumerate(SEQS):
            lo = max(t0, s0)
            hi = min(t0 + n, s0 + L)
            if hi > lo:
                out.append((si, lo - s0, lo - t0, hi - lo))
        return out

    def conformer(self, i):
        S, A = self.S, self.A
        mk = A.mark()
        regB = A.alloc("regB", [NCH, T], BF16)
        mk2 = A.mark()
        PADW = 15
        upoff = []
        off = 0
        for (s0, L, _) in SEQS:
            upoff.append(off)
            off += L + 2 * PADW
        up = A.alloc("upad", [NCH, off], BF16)
        mk3 = A.mark()
        S.op("dve", lambda e: e.memset(up[:], 0.0), writes=[up])
        hT = regB
        hTb = [S.buf("hTt%d" % ti, regB.t) for ti in range(len(TILES))]
        ring = Ring(self, "cring", 3, 8 * 256)
        w = self.dr["cf_w_pw1"].rearrange("(kc p) n -> p kc n", p=128)
        kk = 0
        for c in range(NCH):
            slot = ring.next()
            view = slot.t[:, 0:8 * 256].rearrange("p (k c) -> p k c", k=8)
            S.dma("pool", view[:, :, 0:128], w[:, :, c * 128:(c + 1) * 128], writes=[slot])
            S.dma("pool", view[:, :, 128:256], w[:, :, D + c * 128:D + (c + 1) * 128], writes=[slot])
            for ti, (t0, n, col) in enumerate(TILES):
                if c == 0:
                    self.norm_mod(i, 0, t0, n, col, hTb[ti], t0)
                pa = self.ps[kk % 2]
                pg = self.ps[2 + kk % 2]
                kk += 1
                for kc in range(NCH):
                    self.mm(pa, pa[:, 0:n], view[:, kc, 0:128], hT[:, kc, t0:t0 + n], kc == 0, kc == NCH - 1,
                            [slot, hTb[ti]])
                for kc in range(NCH):
                    self.mm(pg, pg[:, 0:n], view[:, kc, 128:256], hT[:, kc, t0:t0 + n], kc == 0, kc == NCH - 1,
                            [slot, hTb[ti]])
                tf = self.next_tf()
                self.act(tf[:, 0:n], pg[:, 0:n], AF.Sigmoid, [pg, self.vtabB], [tf], bias=self.vcol("cf_b1", 8 + c, 1))
                for (si, pos, cl, ln) in self.seq_pieces(t0, n):
                    d0 = upoff[si] + PADW + pos
                    self.stt(up[:, c, d0:d0 + ln], pa[:, cl:cl + ln], self.vcol("cf_b1", c, 1), tf[:, cl:cl + ln],
                             ALU.add, ALU.mult, [pa, tf, self.vtabB], [up])
        S.barrier()
        A.release(mk3)
        cv = regB
        diag = [A.alloc("diag%d" % j, [31, 128], BF16) for j in range(2)]
        kk = 0
        for c in range(NCH):
            dg = diag[c % 2]
            for k in range(31):
                self.ts(dg[:, k, :], self.identb[:], self.vcol("cf_dw%d" % k, c, 1), None, ALU.mult, None,
                        [self.identb, self.vtabB], [dg])
            for si, (s0, L, _) in enumerate(SEQS):
                for q0 in range(0, L, 512):
                    n = min(512, L - q0)
                    ps = self.ps[kk % 4]
                    kk += 1
                    for k in range(31):
                        b0 = upoff[si] + q0 + k
                        self.mm(ps, ps[:, 0:n], dg[:, k, :], up[:, c, b0:b0 + n], k == 0, k == 30, [dg, up])
                    self.act(cv[:, c, s0 + q0:s0 + q0 + n], ps[:, 0:n], AF.Identity, [ps, self.vtabB], [cv],
                             bias=self.vcol("cf_dwb", c, 1))
        S.barrier()
        A.release(mk2)
        mean = A.alloc("lnmean", [512], F32)
        var = A.alloc("lnvar", [512], F32)
        lt1 = A.alloc("lnt1", [512], F32)
        lt2 = A.alloc("lnt2", [512], F32)
        pm = self.ps[5]
        pv = self.ps[7]
        cvb = [S.buf("cvt%d" % ti, regB.t) for ti in range(len(TILES))]

        def ln_gen(ti):
            (t0, n, col) = TILES[ti]
            cb = cvb[ti]
            for c in range(NCH):
                self.mm(pm, pm[:, 0:n], self.onesb[:], cv[:, c, t0:t0 + n], c == 0, c == NCH - 1, [self.onesb, cb])
            for c in range(NCH):
                sq = self.next_sq()
                self.act(sq[:, 0:n], cv[:, c, t0:t0 + n], AF.Square, [cb], [sq])
                self.mm(pv, pv[:, 0:n], self.onesb[:], sq[:, 0:n], c == 0, c == NCH - 1, [self.onesb, sq])
            yield
            self.act(mean[:, 0:n], pm[:, 0:n], AF.Copy, [pm], [mean], scale=1.0 / D)
            self.tt(lt1[:, 0:n], mean[:, 0:n], mean[:, 0:n], ALU.mult, [mean], [lt1])
            self.stt(var[:, 0:n], pv[:, 0:n], 1.0 / D, lt1[:, 0:n], ALU.mult, ALU.subtract, [pv, lt1], [var])
            self.act(self.rt[:, 0:n], var[:, 0:n], AF.Ln, [var, self.epsb], [self.rt], bias=self.epsb[:, 0:1])
            self.act(self.rstd[:, 0:n], self.rt[:, 0:n], AF.Exp, [self.rt], [self.rstd], scale=-0.5)
            self.copy("dve", var[:, 0:n], self.rstd[:, 0:n], [self.rstd], [var])
            yield
            for c in range(NCH):
                self.tt(lt1[:, 0:n], cv[:, c, t0:t0 + n], mean[:, 0:n], ALU.subtract, [cb, mean], [lt1])
                self.stt(lt2[:, 0:n], lt1[:, 0:n], self.vcol("cf_lng", c, 1), var[:, 0:n], ALU.mult, ALU.mult,
                         [lt1, var, self.vtabB], [lt2])
                self.act(cv[:, c, t0:t0 + n], lt2[:, 0:n], AF.Silu, [lt2, self.vtabB], [cb],
                         bias=self.vcol("cf_lnb", c, 1))
                yield

        self.drain(ln_gen(0), 1000)
        self.outproj(i, cv, "cf_w_pw2", "cf_b2", in_bufs=cvb, pre_gen=ln_gen)
        A.release(mk)

    def shortconv(self, i):
        S, A = self.S, self.A
        mk = A.mark()
        sT = A.alloc("sT", [NCH, T], BF16)
        mk2 = A.mark()
        hT = A.alloc("hT", [NCH, T], BF16)
        mpoff = [s0 + 2 * si for si, (s0, L, _) in enumerate(SEQS)]
        mp = A.alloc("mp", [T + 6], BF16)
        dgs = [A.alloc("scdg%d" % j, [3, 128], BF16) for j in range(2)]
        cvt = [A.alloc("cvt%d" % j, [512], F32) for j in range(2)]
        S.op("dve", lambda e: e.memset(mp[:], 0.0), writes=[mp])
        hTb = [S.buf("hTs%d" % ti, hT.t) for ti in range(len(TILES))]
        ring = Ring(self, "sring", 2, 8 * 384)
        w = self.dr["sc_w_in"].rearrange("(kc p) n -> p kc n", p=128)
        kk = 0
        for c in range(NCH):
            slot = ring.next()
            view = slot.t[:, 0:8 * 384].rearrange("p (k c) -> p k c", k=8)
            for g in range(3):
                S.dma("pool", view[:, :, g * 128:(g + 1) * 128], w[:, :, g * D + c * 128:g * D + (c + 1) * 128],
                      writes=[slot])
            dg = dgs[c % 2]
            for k in range(3):
                self.ts(dg[:, k, :], self.identb[:], self.vcol("sc_cw%d" % k, c, 1), None, ALU.mult, None,
                        [self.identb, self.vtabB], [dg])
            for ti, (t0, n, col) in enumerate(TILES):
                if c == 0:
                    self.norm_mod(i, 0, t0, n, col, hTb[ti], t0)
                pc = self.ps[kk % 2]
                ph = self.ps[2 + kk % 2]
                kk += 1
                for kc in range(NCH):
                    self.mm(pc, pc[:, 0:n], view[:, kc, 128:256], hT[:, kc, t0:t0 + n], kc == 0, kc == NCH - 1,
                            [slot, hTb[ti]])
                for kc in range(NCH):
                    self.mm(ph, ph[:, 0:n], view[:, kc, 256:384], hT[:, kc, t0:t0 + n], kc == 0, kc == NCH - 1,
                            [slot, hTb[ti]])
                tf = self.next_tf()
                self.copy("act", tf[:, 0:n], pc[:, 0:n], [pc], [tf])
                for (si, pos, cl, ln) in self.seq_pieces(t0, n):
                    d0 = mpoff[si] + 1 + pos
                    self.tt(mp[:, d0:d0 + ln], tf[:, cl:cl + ln], ph[:, cl:cl + ln], ALU.mult, [tf, ph], [mp])
            for ti, (t0, n, col) in enumerate(TILES):
                pb = self.ps[4 + kk % 2]
                pcv = self.ps[6 + kk % 2]
                kk += 1
                for kc in range(NCH):
                    self.mm(pb, pb[:, 0:n], view[:, kc, 0:128], hT[:, kc, t0:t0 + n], kc == 0, kc == NCH - 1,
                            [slot, hTb[ti]])
                for (si, pos, cl, ln) in self.seq_pieces(t0, n):
                    for k in range(3):
                        b0 = mpoff[si] + pos + k
                        self.mm(pcv, pcv[:, cl:cl + ln], dg[:, k, :], mp[:, b0:b0 + ln], k == 0, k == 2, [dg, mp])
                cv = cvt[kk % 2]
                self.copy("act", cv[:, 0:n], pcv[:, 0:n], [pcv], [cv])
                self.tt(sT[:, c, t0:t0 + n], cv[:, 0:n], pb[:, 0:n], ALU.mult, [cv, pb], [sT])
        S.barrier()
        A.release(mk2)
        self.outproj(i, sT, "sc_w_out", None)
        A.release(mk)

    def mla(self, i):
        S, A = self.S, self.A
        SCALE = 192.0 ** -0.5
        NK = T + PAST
        mk = A.mark()
        regB = A.alloc("regB", [NCH, T], BF16)
        mk1 = A.mark()
        cqn = A.alloc("cqn", [3, T], BF16)
        ckvT = A.alloc("ckvT", [2, NK], BF16)
        kpeT = A.alloc("kpeT", [NK], BF16)
        S.op("dve", lambda e: e.memset(kpeT[:], 0.0), writes=[kpeT])
        mk2 = A.mark()
        psb = [self.ps[j].t[:, :].bitcast(BF16) for j in range(8)]
        hT = regB
        hTb = [S.buf("hTm%d" % ti, regB.t) for ti in range(len(TILES))]
        ring1 = Ring(self, "m1ring", 2, 8 * 384)
        slot, view = self.wload(ring1, self.dr["mla_w_dq"].rearrange("(kc p) n -> p kc n", p=128), 8, 384)
        for ti, (t0, n, col) in enumerate(TILES):
            self.norm_mod(i, 0, t0, n, col, hTb[ti], t0)
        for ti, (t0, n, col) in enumerate(TILES):
            pq = [self.ps[(ti % 2) * 3 + c] for c in range(3)]
            for c in range(3):
                for kc in range(NCH):
                    self.mm(pq[c], pq[c][:, 0:n], view[:, kc, c * 128:(c + 1) * 128], hT[:, kc, t0:t0 + n], kc == 0,
                            kc == NCH - 1, [slot, hTb[ti]])
            self.stats([pq[c][:, 0:n] for c in range(3)], pq, n, 384)
            for c in range(3):
                self.stt(cqn[:, c, t0:t0 + n], pq[c][:, 0:n], self.vcol("gq", c, 1), self.rstd[:, 0:n], ALU.mult,
                         ALU.mult, [pq[c], self.rstd, self.vtabB], [cqn])
        if MLA_STAGE == 11:
            S.barrier()
            A.release(mk)
            return
        slotkv, viewkv = self.wload(ring1, self.dr["mla_w_dkv"].rearrange("(kc p) n -> p kc n", p=128), 8, 320)
        gkv = A.alloc("gkv", [256], F32)
        S.dma("sp", gkv[:], self.dr["mla_g_kv"][0:1, :].partition_broadcast(128), writes=[gkv])
        cosk = A.alloc("cosk", [16, 32], F32)
        sink = A.alloc("sink", [16, 32], F32)
        for tb in range(16):
            S.dma("sp", cosk[:, tb, :], self.dr["c_cosK"][tb * 128:(tb + 1) * 128, :], writes=[cosk])
            S.dma("sp", sink[:, tb, :], self.dr["c_sinK"][tb * 128:(tb + 1) * 128, :], writes=[sink])
        NB = 4
        kvst = [A.alloc("kvst%d" % j, [320], F32) for j in range(NB)]
        kvb = [A.alloc("kvb%d" % j, [320], BF16) for j in range(NB)]
        rps = [A.alloc("rp%d" % j, [4, 32], F32) for j in range(NB)]
        ssqs = [A.alloc("ssq%d" % j, [4], F32) for j in range(NB)]

        def cast_transpose(st, kb_, kc0, pidx):
            self.copy("dve", kb_[:, :], st[:, :], [st], [kb_])
            pb = self.ps[pidx]
            pv = psb[pidx]
            S.op("pe", lambda e: e.transpose(pv[:, 0:128], kb_[:, 0:128], self.identb[:]), reads=[kb_, self.identb],
                 writes=[pb])
            S.op("pe", lambda e: e.transpose(pv[:, 128:256], kb_[:, 128:256], self.identb[:]),
                 reads=[kb_, self.identb], writes=[pb])
            S.op("pe", lambda e: e.transpose(pv[0:64, 256:384], kb_[:, 256:320], self.identb[:]),
                 reads=[kb_, self.identb], writes=[pb])
            self.copy("dve", ckvT[:, :, kc0:kc0 + 128], pv[:, 0:256].rearrange("p (a b) -> p a b", a=2), [pb], [ckvT])
            self.copy("dve", kpeT[0:64, kc0:kc0 + 128], pv[0:64, 256:384], [pb], [kpeT])

        for b in range(2):
            st = kvst[b]
            S.dma("sp", st[:, 0:256], self.dr["cckv"][b * 128:(b + 1) * 128, :], writes=[st])
            S.dma("sp", st[:, 256:320], self.dr["ckpe"][b * 128:(b + 1) * 128, :], writes=[st])
            cast_transpose(st, kvb[b], b * 128, 4 + b)
        if MLA_STAGE == 12:
            S.barrier()
            A.release(mk)
            return
        for tb in range(T // 128):
            tok0 = tb * 128
            ps = self.ps[tb % NB]
            rp = rps[tb % NB]
            ssq = ssqs[tb % NB]
            for kc in range(NCH):
                self.mm(ps, ps[:, 0:320], hT[:, kc, tok0:tok0 + 128], viewkv[:, kc, :], kc == 0, kc == NCH - 1,
                        [slotkv, hTb[min(tok0 // 512, 4)]])
            tf = self.next_tf()
            self.act(tf[:, 0:256], ps[:, 0:256], AF.Square, [ps], [tf, ssq], accum_out=ssq[:, 0:1])
            self.act(ssq[:, 1:2], ssq[:, 0:1], AF.Sqrt, [ssq, self.epsb], [ssq], bias=self.epsb[:, 0:1],
                     scale=1.0 / 256)
            S.op("dve", lambda e, ssq=ssq: e.reciprocal(ssq[:, 2:3], ssq[:, 1:2]), reads=[ssq], writes=[ssq])
            st = kvst[tb % NB]
            self.stt(st[:, 0:256], ps[:, 0:256], ssq[:, 2:3], gkv[:], ALU.mult, ALU.mult, [ps, ssq, gkv], [st])
            if tok0 >= LS or MLA_STAGE == 13:
                self.copy("act", st[:, 256:320], ps[:, 256:320], [ps], [st])
                r0 = tok0 - LS
                S.dma("act", self.ckv_out[r0:r0 + 128, :], st[:, 0:256], reads=[st], writes=[self.CKVO], dbuf=st)
                S.dma("act", self.kpe_out[r0:r0 + 128, :], st[:, 256:320], reads=[st], writes=[self.CKVO], dbuf=st)
            else:
                x = ps[:, 256:320].rearrange("p (a s f) -> p a s f", a=2, s=2)
                o = st[:, 256:320].rearrange("p (a s f) -> p a s f", a=2, s=2)
                cs = cosk[:, tb, :].rearrange("p (a f) -> p a f", a=2)
                sn = sink[:, tb, :].rearrange("p (a f) -> p a f", a=2)
                r = [rp[:, j, :].rearrange("p (a f) -> p a f", a=2) for j in range(4)]
                self.tt(r[0], x[:, :, 0, :], cs, ALU.mult, [ps, cosk], [rp])
                self.tt(r[1], x[:, :, 1, :], sn, ALU.mult, [ps, sink], [rp])
                self.tt(r[2], x[:, :, 0, :], sn, ALU.mult, [ps, sink], [rp])
                self.tt(r[3], x[:, :, 1, :], cs, ALU.mult, [ps, cosk], [rp])
                self.tt(o[:, :, 0, :], r[0], r[1], ALU.subtract, [rp], [st])
                self.tt(o[:, :, 1, :], r[2], r[3], ALU.add, [rp], [st])
            if MLA_STAGE != 14:
                cast_transpose(st, kvb[tb % NB], tok0 + PAST, 4 + tb % NB)
        S.barrier()
        A.release(mk2)
        if MLA_STAGE in (1, 13, 14):
            A.release(mk)
            return
        oT = regB
        ring2 = Ring(self, "m2ring", 3, 1280)
        knT = A.alloc("knT", [NK], BF16)
        V = A.alloc("V", [NK // 128, 128], BF16)
        qn = [A.alloc("qn%d" % j, [512], BF16) for j in range(2)]
        qp = [A.alloc("qp%d" % j, [512], BF16) for j in range(2)]
        Pt = [A.alloc("Pt%d" % j, [512], BF16) for j in range(3)]
        for b_ in qp:
            S.op("dve", lambda e, b_=b_: e.memset(b_[:], 0.0), writes=[b_])
        cosq = A.alloc("cosq", [512], F32)
        sinq = A.alloc("sinq", [512], F32)
        bnd = A.alloc("bnd", [16], F32)
        psq = [A.alloc("psq%d" % j, [512], BF16) for j in range(2)]
        ptf = [A.alloc("ptf%d" % j, [512], F32) for j in range(2)]
        wuq = self.dr["mla_w_uq"].rearrange("(kc p) n -> p kc n", p=128)
        wsw = self.dr["mla_w_uq_sw"].rearrange("(kc p) n -> p kc n", p=128)
        wkv = self.dr["mla_w_ukv"].rearrange("(kc p) n -> p kc n", p=128)
        groups = [(qt * 512, 512, list(range(0, 18)), True) for qt in range(4)]
        groups.append((LS, LP, [18, 19], False))
        groups.append((LS + LP, LP, [20, 21], False))
        gi = 0
        for h in range(8):
            slot = ring2.next()
            uq = slot.t[:, 0:576].rearrange("p (k c) -> p k c", k=3)
            sw = slot.t[:, 576:768].rearrange("p (k c) -> p k c", k=3)
            ukv = slot.t[:, 768:1280].rearrange("p (k c) -> p k c", k=2)
            S.dma("pool", uq, wuq[:, :, h * 192:(h + 1) * 192], writes=[slot])
            S.dma("pool", sw, wsw[:, :, h * 64:(h + 1) * 64], writes=[slot])
            S.dma("pool", ukv, wkv[:, :, h * 256:(h + 1) * 256], writes=[slot])
            first = True
            for kt in (range(0, NK, 512) if MLA_STAGE != 22 else []):
                n = min(512, NK - kt)
                ps = self.ps[(kt // 512) % 2]
                for kc in range(2):
                    self.mm(ps, ps[:, 0:n], ukv[:, kc, 0:128], ckvT[:, kc, kt:kt + n], kc == 0, kc == 1, [slot, ckvT])
                self.copy("dve", knT[:, kt:kt + n], ps[:, 0:n], [ps], [knT])
                sq = self.next_sq()
                self.act(sq[:, 0:n], ps[:, 0:n], AF.Square, [ps], [sq])
                sq2 = self.next_sq()
                self.act(sq2[0:64, 0:n], kpeT[0:64, kt:kt + n], AF.Square, [kpeT], [sq2])
                pb = self.ps[6]
                self.mm(pb, pb[:, 0:n], self.onesb[:], sq[:, 0:n], True, False, [self.onesb, sq])
                self.mm(pb, pb[:, 0:n], self.onesb[0:64, :], sq2[0:64, 0:n], False, True, [self.onesb, sq2])
                if first:
                    S.op("dve", lambda e, n=n, pb=pb: e.reduce_max(bnd[:, 0:1], pb[:, 0:n], AX.X), reads=[pb],
                         writes=[bnd])
                    first = False
                else:
                    S.op("dve", lambda e, n=n, pb=pb: e.reduce_max(bnd[:, 1:2], pb[:, 0:n], AX.X), reads=[pb],
                         writes=[bnd])
                    self.tt(bnd[:, 0:1], bnd[:, 0:1], bnd[:, 1:2], ALU.max, [bnd], [bnd])
            for kb0 in (range(0, NK // 128, 4) if MLA_STAGE != 21 else []):
                nb = min(4, NK // 128 - kb0)
                ps = self.ps[2 + (kb0 // 4) % 2]
                for j in range(nb):
                    kb = kb0 + j
                    for kc in range(2):
                        self.mm(ps, ps[:, j * 128:(j + 1) * 128], ckvT[:, kc, kb * 128:(kb + 1) * 128],
                                ukv[:, kc, 128:256], kc == 0, kc == 1, [slot, ckvT])
                self.copy("act", V[:, kb0:kb0 + nb, :], ps[:, 0:nb * 128].rearrange("p (j d) -> p j d", j=nb), [ps],
                          [V])
            def prep(g, slot=slot, uq=uq, sw=sw):
                (tok0, nq, kbs, is_s) = groups[g]
                par = g % 2
                qnb = qn[par]
                qpb = qp[par]
                b0 = 2 + 4 * par
                pn = self.ps[6]
                pp = self.ps[7]
                for kc in range(3):
                    self.mm(pn, pn[:, 0:nq], uq[:, kc, 0:128], cqn[:, kc, tok0:tok0 + nq], kc == 0, kc == 2,
                            [slot, cqn])
                for kc in range(3):
                    self.mm(pp, pp[0:64, 0:nq], uq[:, kc, 128:192], cqn[:, kc, tok0:tok0 + nq], kc == 0, kc == 2,
                            [slot, cqn])
                self.copy("dve", qnb[:, 0:nq], pn[:, 0:nq], [pn], [qnb])
                sq = psq[0]
                self.act(sq[:, 0:nq], pn[:, 0:nq], AF.Square, [pn], [sq])
                sq2 = psq[1]
                self.act(sq2[0:64, 0:nq], pp[0:64, 0:nq], AF.Square, [pp], [sq2])
                if is_s:
                    S.dma("sp", cosq[0:64, 0:nq], self.dr["c_cosQ"][:, tok0:tok0 + nq], writes=[cosq])
                    S.dma("sp", sinq[0:64, 0:nq], self.dr["c_sinQ"][:, tok0:tok0 + nq], writes=[sinq])
                    t1 = ptf[0]
                    self.tt(t1[0:64, 0:nq], pp[0:64, 0:nq], cosq[0:64, 0:nq], ALU.mult, [pp, cosq], [t1])
                    yield
                    for kc in range(3):
                        self.mm(pp, pp[0:64, 0:nq], sw[:, kc, :], cqn[:, kc, tok0:tok0 + nq], kc == 0, kc == 2,
                                [slot, cqn])
                    t2 = ptf[1]
                    self.tt(t2[0:64, 0:nq], pp[0:64, 0:nq], sinq[0:64, 0:nq], ALU.mult, [pp, sinq], [t2])
                    self.tt(qpb[0:64, 0:nq], t1[0:64, 0:nq], t2[0:64, 0:nq], ALU.add, [t1, t2], [qpb])
                else:
                    self.copy("dve", qpb[0:64, 0:nq], pp[0:64, 0:nq], [pp], [qpb])
                    yield
                pb = self.ps[6]
                self.mm(pb, pb[:, 0:nq], self.onesb[:], sq[:, 0:nq], True, False, [self.onesb, sq])
                self.mm(pb, pb[:, 0:nq], self.onesb[0:64, :], sq2[0:64, 0:nq], False, True, [self.onesb, sq2])
                S.op("dve", lambda e, nq=nq, pb=pb: e.reduce_max(bnd[:, b0:b0 + 1], pb[:, 0:nq], AX.X), reads=[pb],
                     writes=[bnd])
                self.tt(bnd[:, b0 + 1:b0 + 2], bnd[:, b0:b0 + 1], bnd[:, 0:1], ALU.mult, [bnd], [bnd])
                self.act(bnd[:, b0 + 2:b0 + 3], bnd[:, b0 + 1:b0 + 2], AF.Sqrt, [bnd], [bnd])
                self.ts(bnd[:, b0 + 3:b0 + 4], bnd[:, b0 + 2:b0 + 3], -SCALE, None, ALU.mult, None, [bnd], [bnd])
                yield

            ngr = len(groups) if MLA_STAGE not in (2, 21, 22) else 0
            if ngr:
                self.drain(prep(0), 10)
            for g in range(ngr):
                (tok0, nq, kbs, is_s) = groups[g]
                par = g % 2
                qnb = qn[par]
                qpb = qp[par]
                negM = bnd[:, 2 + 4 * par + 3:2 + 4 * par + 4]
                pg_ = prep(g + 1) if g + 1 < ngr else None
                po = self.ps[4]
                pd = self.ps[5]
                nkb = len(kbs)
                LA = 2
                for j in range(nkb + LA):
                    if j in (1, 8):
                        self.drain(pg_, 1)
                    if j < nkb:
                        kb = kbs[j]
                        pS = self.ps[j % 4]
                        self.mm(pS, pS[:, 0:nq], knT[:, kb * 128:(kb + 1) * 128], qnb[:, 0:nq], True, False,
                                [knT, qnb])
                        self.mm(pS, pS[:, 0:nq], kpeT[:, kb * 128:(kb + 1) * 128], qpb[:, 0:nq], False, True,
                                [kpeT, qpb])
                        P_ = Pt[j % 3]
                        self.act(P_[:, 0:nq], pS[:, 0:nq], AF.Exp, [pS, bnd], [P_], bias=negM, scale=SCALE)
                    if j >= LA:
                        kb = kbs[j - LA]
                        P_ = Pt[(j - LA) % 3]
                        self.mm(po, po[:, 0:nq], V[:, kb, :], P_[:, 0:nq], j == LA, j == nkb + LA - 1, [V, P_])
                        self.mm(pd, pd[:, 0:nq], self.onesb[:], P_[:, 0:nq], j == LA, j == nkb + LA - 1,
                                [self.onesb, P_])
                self.drain(pg_, 10)
                tf0 = self.next_tf()
                self.act(tf0[:, 0:nq], pd[:, 0:nq], AF.Ln, [pd], [tf0])
                tf = self.next_tf()
                self.act(tf[:, 0:nq], tf0[:, 0:nq], AF.Exp, [tf0], [tf], scale=-1.0)
                self.tt(oT[:, h, tok0:tok0 + nq], po[:, 0:nq], tf[:, 0:nq], ALU.mult, [po, tf], [oT])
        S.barrier()
        A.release(mk1)
        self.outproj(i, oT, "mla_w_o", None)
        A.release(mk)

    def alloc_at(self, name, off, shape, dt, parts=128):
        esz = 4 if dt == F32 else 2
        n = int(np.prod(shape)) * esz
        ap = self.A.t[0:parts, off:off + n].bitcast(dt)
        if len(shape) == 2:
            ap = ap.rearrange("p (a b) -> p a b", a=shape[0])
        elif len(shape) == 3:
            ap = ap.rearrange("p (a b c) -> p a b c", a=shape[0], b=shape[1])
        return self.S.buf(name, ap)

    def sin_act(self, dst, dstb, src_ps, nparts, n, scale_ap, bias_ap, rd):
        PI = float(np.pi)
        a, ta_, tb_ = self.sa
        self.act(a[0:nparts, 0:n], src_ps, AF.Identity, rd, [a], bias=bias_ap, scale=scale_ap)
        for _ in range(2):
            t1 = ta_
            self.ts(t1[0:nparts, 0:n], a[0:nparts, 0:n], PI, 2 * PI, ALU.is_gt, ALU.mult, [a], [t1])
            self.tt(a[0:nparts, 0:n], a[0:nparts, 0:n], t1[0:nparts, 0:n], ALU.subtract, [a, t1], [a])
            t2 = tb_
            self.ts(t2[0:nparts, 0:n], a[0:nparts, 0:n], -PI, 2 * PI, ALU.is_lt, ALU.mult, [a], [t2])
            self.tt(a[0:nparts, 0:n], a[0:nparts, 0:n], t2[0:nparts, 0:n], ALU.add, [a, t2], [a])
        self.act(dst, a[0:nparts, 0:n], AF.Sin, [a], [dstb])

    def hyena_filters(self):
        S, A = self.S, self.A
        mk = A.mark()
        xo = self.xT_off
        sumb = self.alloc_at("hsum", xo, [16, D], BF16)
        difb = self.alloc_at("hdif", xo + 32768, [16, D], BF16)
        w3 = self.alloc_at("hw3", xo + 65536, [4 * D], F32, parts=64)
        w1 = A.alloc("hw1", [64], F32)
        w2 = A.alloc("hw2", [64], F32)
        zT = A.alloc("hzT", [LS], F32)
        h1T = A.alloc("hh1T", [LS], F32)
        h2T = A.alloc("hh2T", [LS], F32)
        absd = A.alloc("habsd", [D], F32)
        dec = A.alloc("hdec", [D], F32)
        skipo = A.alloc("hskip", [D], F32)
        hst = [A.alloc("hst%d" % j, [2, D], F32) for j in range(2)]
        fring = [A.alloc("hfr%d" % j, [2, 16, 128], BF16) for j in range(2)]
        negt = A.alloc("hnegt", [16], F32)
        scl = A.alloc("hscl", [16], F32)
        nyq = A.alloc("hnyq", [16], BF16)
        fb = A.alloc("hfb", [2], F32)
        mring0 = Ring(self, "mring0", 2, 8 * 128)
        modg = self.modulation_gen(0, mring0, psi=5)
        self.sa = [A.alloc("hsa%d" % j, [512], F32) for j in range(3)]
        S.dma("sp", w3[:], self.dr["hy_f_w3"][:, :], writes=[w3])
        S.dma("sp", w1[0:33, :], self.dr["hy_f_w1"][:, :], writes=[w1])
        S.dma("sp", w2[0:64, :], self.dr["hy_f_w2"][:, :], writes=[w2])
        S.dma("sp", absd[:], self.dr["c_absdelta"][0:1, :].partition_broadcast(128), writes=[absd])
        fq0 = self.vcol("hy_fq0")
        fq1 = self.vcol("hy_fq1")
        self.tt(fb[0:64, 0:1], fq0[0:64, :], self.vcol("hy_fb1")[0:64, :], ALU.mult, [self.vtabB], [fb])
        self.tt(fb[0:64, 1:2], fq1[0:64, :], self.vcol("hy_fb2")[0:64, :], ALU.mult, [self.vtabB], [fb])
        kk = 0
        hsti = 0
        for L in (LS, LP):
            TC = L // 128
            FC = TC
            S.dma("sp", zT[0:33, 0:L], self.dr["c_zT%d" % L][:, :], writes=[zT])
            S.dma("sp", negt[:, 0:TC], self.dr["c_negt%d" % L][:, :], writes=[negt])
            S.dma("sp", scl[:, 0:FC], self.dr["c_dscl%d" % L][:, :], writes=[scl])
            S.dma("sp", nyq[:, 0:TC], self.dr["c_nyq%d" % L][:, :], writes=[nyq])
            for q0 in range(0, L, 512):
                n = min(512, L - q0)
                ps = self.ps[kk % 4]
                kk += 1
                self.mm(ps, ps[0:64, 0:n], w1[0:33, :], zT[0:33, q0:q0 + n], True, True, [w1, zT])
                self.sin_act(h1T[0:64, q0:q0 + n], h1T, ps[0:64, 0:n], 64, n, fq0[0:64, :], fb[0:64, 0:1],
                             [ps, self.vtabB, fb])
            for q0 in range(0, L, 512):
                n = min(512, L - q0)
                ps = self.ps[kk % 4]
                kk += 1
                self.mm(ps, ps[0:64, 0:n], w2[0:64, :], h1T[0:64, q0:q0 + n], True, True, [w2, h1T])
                self.sin_act(h2T[0:64, q0:q0 + n], h2T, ps[0:64, 0:n], 64, n, fq1[0:64, :], fb[0:64, 1:2],
                             [ps, self.vtabB, fb])
            for o in range(2):
                S.dma("sp", skipo[0:1, :], self.dr["hy_skip"][o:o + 1, :], writes=[skipo])
                for tb in range(TC):
                    self.act(dec[:], absd[:], AF.Exp, [absd, negt], [dec], scale=negt[:, tb:tb + 1])
                    for ch in range(2):
                        pf = self.ps[kk % 2]
                        pbk = self.ps[2 + kk % 2]
                        kk += 1
                        cf = o * D + ch * 512
                        cb = 2 * D + o * D + ch * 512
                        self.mm(pf, pf[:, :], h2T[0:64, tb * 128:(tb + 1) * 128], w3[0:64, cf:cf + 512], True, True,
                                [h2T, w3])
                        self.mm(pbk, pbk[:, :], h2T[0:64, tb * 128:(tb + 1) * 128], w3[0:64, cb:cb + 512], True, True,
                                [h2T, w3])
                        tb_ = self.next_tf()
                        self.copy("act", tb_[:, :], pbk[:, :], [pbk], [tb_])
                        s1 = self.next_tf()
                        self.tt(s1[:, :], pf[:, :], tb_[:, :], ALU.add, [pf, tb_], [s1])
                        if tb == 0:
                            self.tt(s1[0:1, :], s1[0:1, :], skipo[0:1, ch * 512:(ch + 1) * 512], ALU.add, [s1, skipo],
                                    [s1])
                        self.tt(sumb[:, tb, ch * 512:(ch + 1) * 512], s1[:, :], dec[:, ch * 512:(ch + 1) * 512],
                                ALU.mult, [s1, dec], [sumb])
                        d1 = self.next_tf()
                        self.tt(d1[:, :], pf[:, :], tb_[:, :], ALU.subtract, [pf, tb_], [d1])
                        self.tt(difb[:, tb, ch * 512:(ch + 1) * 512], d1[:, :], dec[:, ch * 512:(ch + 1) * 512],
                                ALU.mult, [d1, dec], [difb])
                        if tb == 0:
                            self.copy("dve", difb[0:1, 0, ch * 512:(ch + 1) * 512],
                                      sumb[0:1, 0, ch * 512:(ch + 1) * 512], [sumb], [difb])
                for fc in range(FC):
                    fr = fring[fc % 2]
                    S.dma("sp", fr[:, :, 0:TC, :], self.dr["c_dftF%d" % L][:, fc].rearrange("m p t f -> p m t f"),
                          writes=[fr])
                    hs = hst[hsti % 2]
                    hsti += 1
                    for ch in range(2):
                        pc = self.ps[kk % 2]
                        pn = self.ps[2 + kk % 2]
                        kk += 1
                        for tc in range(TC):
                            self.mm(pc, pc[:, :], fr[:, 0, tc, :], sumb[:, tc, ch * 512:(ch + 1) * 512], tc == 0,
                                    tc == TC - 1, [fr, sumb])
                        for tc in range(TC):
                            self.mm(pn, pn[:, :], fr[:, 1, tc, :], difb[:, tc, ch * 512:(ch + 1) * 512], tc == 0,
                                    tc == TC - 1, [fr, difb])
                        self.act(hs[:, 0, ch * 512:(ch + 1) * 512], pc[:, :], AF.Copy, [pc, scl], [hs],
                                 scale=scl[:, fc:fc + 1])
                        self.act(hs[:, 1, ch * 512:(ch + 1) * 512], pn[:, :], AF.Copy, [pn, scl], [hs],
                                 scale=scl[:, fc:fc + 1])
                        if fc == 0:
                            py = self.ps[7]
                            for tc in range(TC):
                                self.mm(py, py[0:1, :], nyq[:, tc:tc + 1], sumb[:, tc, ch * 512:(ch + 1) * 512],
                                        tc == 0, tc == TC - 1, [nyq, sumb])
                            self.act(hs[0:1, 1, ch * 512:(ch + 1) * 512], py[0:1, :], AF.Copy, [py, scl], [hs],
                                     scale=scl[0:1, 0:1])
                    S.dma("act", self.htab[L][o, :, fc].rearrange("m p c -> p m c"), hs[:, :, :], reads=[hs],
                          writes=[self.HT[L]], dbuf=hs)
                    self.drain(modg, 2)
        self.drain(modg, 1000)
        S.barrier()
        A.release(mk)

    def hyena_layer0(self):
        S, A = self.S, self.A
        self.hyena_filters()
        i = 0
        mk = A.mark()
        xo = self.xT_off
        vtok = self.alloc_at("vtok", xo, [T // 128, D], BF16)
        x1T = self.alloc_at("x1T", xo + 40960, [NCH, T], BF16)
        x2T = A.alloc("x2T", [NCH, T], BF16)
        mk2 = A.mark()
        hT = A.alloc("hT", [NCH, T], BF16)
        psb = [self.ps[j].t[:, :].bitcast(BF16) for j in range(8)]
        lt = self.lt[0]
        mk3 = A.mark()
        xst = [A.alloc("xst%d" % j, [D], F32) for j in range(2)]
        junks = [A.alloc("xjunk%d" % j, [D], F32) for j in range(2)]
        ssqs = [A.alloc("ssq0%d" % j, [4], F32) for j in range(2)]
        for tb in range(T // 128):
            col = 0 if tb * 128 < LS else 1
            st = xst[tb % 2]
            junk = junks[tb % 2]
            ssq = ssqs[tb % 2]
            S.dma("sp", st[:], self.dr["xs"][tb * 128:(tb + 1) * 128, :], writes=[st])
            self.act(junk[:], st[:], AF.Square, [st], [junk, ssq], accum_out=ssq[:, 0:1])
            self.act(ssq[:, 1:2], ssq[:, 0:1], AF.Sqrt, [ssq, self.epsb], [ssq], bias=self.epsb[:, 0:1], scale=1.0 / D)
            S.op("dve", lambda e, ssq=ssq: e.reciprocal(ssq[:, 2:3], ssq[:, 1:2]), reads=[ssq], writes=[ssq])
            self.ts(st[:], st[:], ssq[:, 2:3], None, ALU.mult, None, [st, ssq], [st])
            for h in range(2):
                ps = self.ps[(tb * 2 + h) % 4]
                for j in range(4):
                    cch = h * 4 + j
                    S.op("pe", lambda e, j=j, cch=cch, ps=ps, st=st: e.transpose(
                        ps[:, j * 128:(j + 1) * 128], st[:, cch * 128:(cch + 1) * 128], self.identf[:]),
                        reads=[st, self.identf], writes=[ps])
                for j in range(4):
                    cch = h * 4 + j
                    if h == 0:
                        self.act(hT[:, cch, tb * 128:(tb + 1) * 128], ps[:, j * 128:(j + 1) * 128], AF.Identity,
                                 [ps, lt], [hT], bias=lt[:, 1, cch, col:col + 1], scale=lt[:, 0, cch, col:col + 1])
                    else:
                        self.ts(hT[:, cch, tb * 128:(tb + 1) * 128], ps[:, j * 128:(j + 1) * 128],
                                lt[:, 0, cch, col:col + 1], lt[:, 1, cch, col:col + 1], ALU.mult, ALU.add, [ps, lt],
                                [hT])
        S.barrier()
        A.release(mk3)
        ppoff = [s0 + 2 * si for si, (s0, L, _) in enumerate(SEQS)]
        pres = [A.alloc("hpre%d" % j, [T + 6], BF16) for j in range(2)]
        vst = A.alloc("hvst", [T], BF16)
        for pre in pres:
            S.op("dve", lambda e, pre=pre: e.memset(pre[:], 0.0), writes=[pre])
        ring = Ring(self, "hring", 2, 8 * 256)
        w = self.dr["hy_w_in"].rearrange("(kc p) n -> p kc n", p=128)
        kk = 0
        for jj in range(12):
            slot, view = self.wload(ring, w[:, :, jj * 256:(jj + 1) * 256], 8, 256)
            for jl in range(2):
                j = jj * 2 + jl
                pre = pres[j % 2]
                for (t0, n, col) in TILES:
                    ps = self.ps[kk % 4]
                    kk += 1
                    for kc in range(NCH):
                        self.mm(ps, ps[:, 0:n], view[:, kc, jl * 128:(jl + 1) * 128], hT[:, kc, t0:t0 + n], kc == 0,
                                kc == NCH - 1, [slot, hT])
                    for (si, pos, cl, ln) in self.seq_pieces(t0, n):
                        d0 = ppoff[si] + 1 + pos
                        self.act(pre[:, d0:d0 + ln], ps[:, cl:cl + ln], AF.Identity, [ps, self.vtabB], [pre],
                                 bias=self.vcol("hy_b_in", j, 1))
                for (t0, n, col) in TILES:
                    ta = self.next_tf()
                    for (si, pos, cl, ln) in self.seq_pieces(t0, n):
                        p0 = ppoff[si] + pos
                        s0 = SEQS[si][0]
                        if j < 8:
                            dst = vst[:, s0 + pos:s0 + pos + ln]
                            db = vst
                        elif j < 16:
                            dst = x1T[:, j - 8, s0 + pos:s0 + pos + ln]
                            db = x1T
                        else:
                            dst = x2T[:, j - 16, s0 + pos:s0 + pos + ln]
                            db = x2T
                        self.ts(ta[:, cl:cl + ln], pre[:, p0:p0 + ln], self.vcol("hy_cw0", j, 1),
                                self.vcol("hy_cb", j, 1), ALU.mult, ALU.add, [pre, self.vtabB], [ta])
                        self.stt(ta[:, cl:cl + ln], pre[:, p0 + 1:p0 + 1 + ln], self.vcol("hy_cw1", j, 1),
                                 ta[:, cl:cl + ln], ALU.mult, ALU.add, [pre, ta, self.vtabB], [ta])
                        self.stt(dst, pre[:, p0 + 2:p0 + 2 + ln], self.vcol("hy_cw2", j, 1), ta[:, cl:cl + ln],
                                 ALU.mult, ALU.add, [pre, ta, self.vtabB], [db])
                if j < 8:
                    self.to_tokmajor(vst, None, vtok, j, psb)
        S.barrier()
        A.release(mk2)
        Y = A.alloc("hyY", [32, 512], BF16)
        hb = [A.alloc("hyH%d" % j, [2, 512], F32) for j in range(2)]
        fring = [A.alloc("hyfr%d" % j, [2, 16, 128], BF16) for j in range(2)]
        iring = [A.alloc("hyir%d" % j, [4, 512], BF16) for j in range(2)]
        cnt = {"f": 0, "i": 0, "h": 0, "k": 0}
        for o in range(2):
            gate = x1T if o == 0 else x2T
            for hf in range(2):
                for (s0, L, _) in SEQS:
                    TC = L // 128
                    FC = TC
                    tb0 = s0 // 128
                    for fc in range(FC):
                        fr = fring[cnt["f"] % 2]
                        cnt["f"] += 1
                        S.dma("sp", fr[:, :, 0:TC, :], self.dr["c_dftF%d" % L][:, fc].rearrange("m p t f -> p m t f"),
                              writes=[fr])
                        h_ = hb[cnt["h"] % 2]
                        cnt["h"] += 1
                        S.dma("sp", h_[:, :, :],
                              self.htab[L][o, :, fc, :, hf * 512:(hf + 1) * 512].rearrange("m p c -> p m c"),
                              reads=[self.HT[L]], writes=[h_])
                        pzc = self.ps[(cnt["k"] % 2) * 2]
                        pzs = self.ps[(cnt["k"] % 2) * 2 + 1]
                        cnt["k"] += 1
                        for tc in range(TC):
                            self.mm(pzc, pzc[:, :], fr[:, 0, tc, :], vtok[:, tb0 + tc, hf * 512:(hf + 1) * 512],
                                    tc == 0, tc == TC - 1, [fr, vtok])
                        for tc in range(TC):
                            self.mm(pzs, pzs[:, :], fr[:, 1, tc, :], vtok[:, tb0 + tc, hf * 512:(hf + 1) * 512],
                                    tc == 0, tc == TC - 1, [fr, vtok])
                        t1 = self.next_tf()
                        t2 = self.next_tf()
                        self.tt(t1[:, :], pzc[:, :], h_[:, 0, :], ALU.mult, [pzc, h_], [t1])
                        self.tt(t2[:, :], pzs[:, :], h_[:, 1, :], ALU.mult, [pzs, h_], [t2])
                        self.tt(Y[:, fc, :], t1[:, :], t2[:, :], ALU.subtract, [t1, t2], [Y])
                        t3 = self.next_tf()
                        self.tt(t3[:, :], pzc[:, :], h_[:, 1, :], ALU.mult, [pzc, h_], [t3])
                        t4 = self.next_tf()
                        self.tt(t4[:, :], pzs[:, :], h_[:, 0, :], ALU.mult, [pzs, h_], [t4])
                        self.tt(Y[:, FC + fc, :], t3[:, :], t4[:, :], ALU.add, [t3, t4], [Y])
                        if fc == 0:
                            self.tt(Y[0:1, 0, :], pzc[0:1, :], h_[0:1, 0, :], ALU.mult, [pzc, h_], [Y])
                            self.tt(Y[0:1, FC, :], pzs[0:1, :], h_[0:1, 1, :], ALU.mult, [pzs, h_], [Y])
                    for q0 in range(0, L, 512):
                        n = min(512, L - q0)
                        po = [self.ps[4 + cc] for cc in range(4)]
                        for kg in range(0, 2 * FC, 4):
                            ir = iring[cnt["i"] % 2]
                            cnt["i"] += 1
                            for kx in range(4):
                                k = kg + kx
                                m_, fcx = k // FC, k % FC
                                S.dma("sp", ir[:, kx, 0:n],
                                      self.dr["c_dftI%d" % L][m_, fcx * 128:(fcx + 1) * 128, q0:q0 + n], writes=[ir])
                            for kx in range(4):
                                k = kg + kx
                                for cc in range(4):
                                    self.mm(po[cc], po[cc][:, 0:n], Y[:, k, cc * 128:(cc + 1) * 128], ir[:, kx, 0:n],
                                            k == 0, k == 2 * FC - 1, [Y, ir])
                        for cc in range(4):
                            ch = hf * 4 + cc
                            g = gate[:, ch, s0 + q0:s0 + q0 + n]
                            self.tt(g, po[cc][:, 0:n], g, ALU.mult, [po[cc], gate], [gate])
            if o == 0:
                for j in range(NCH):
                    self.to_tokmajor(None, x1T, vtok, j, psb)
        S.barrier()
        A.release(mk2)
        self.load_xT()
        self.outproj(0, x2T, "hy_w_out", "hy_b_out")
        A.release(mk)

    def to_tokmajor(self, vst, srcT, vtok, j, psb):
        S = self.S
        for g in range(0, T // 128, 8):
            nb = min(8, T // 128 - g)
            pidx = 4 + (g // 8) % 2 if vst is not None else (g // 8) % 4
            pb = self.ps[pidx]
            pv = psb[pidx]
            for b in range(nb):
                tb = g + b
                if vst is not None:
                    src = vst[:, tb * 128:(tb + 1) * 128]
                    sb_ = vst
                else:
                    src = srcT[:, j, tb * 128:(tb + 1) * 128]
                    sb_ = srcT
                S.op("pe", lambda e, b=b, src=src, pv=pv: e.transpose(pv[:, b * 128:(b + 1) * 128], src, self.identb[:]),
                     reads=[sb_, self.identb], writes=[pb])
            self.copy("dve", vtok[:, g:g + nb, j * 128:(j + 1) * 128],
                      pv[:, 0:nb * 128].rearrange("p (b c) -> p b c", b=nb), [pb], [vtok])

_PROG_CACHE = {}


def _get_prog():
    key = (DEPTH_LIMIT, tuple(MIXERS))
    if key not in _PROG_CACHE:
        _PROG_CACHE[key] = Prog()
    return _PROG_CACHE[key]


def _make_in_maps(inputs):
    inp = {k: np.asarray(v) for k, v in inputs.items()}
    c = _consts()
    rows, _ = _vec_rows(inp)
    shared = {}
    for k in ["w_mod", "ffn_w_gate", "ffn_w_up", "ffn_w_down"]:
        shared[k] = np.ascontiguousarray(inp[k], dtype=np.float32)
    for k in ["hy_w_in", "hy_f_w1", "hy_f_w2", "hy_f_w3", "hy_skip", "hy_w_out", "cf_w_pw1", "cf_w_pw2", "sc_w_in",
              "sc_w_out", "mla_w_dq", "mla_w_uq", "mla_w_dkv", "mla_w_ukv", "mla_w_o"]:
        shared[k] = np.ascontiguousarray(inp[k][0], dtype=np.float32)
    shared["mla_g_kv"] = np.ascontiguousarray(inp["mla_g_kv"][0].reshape(1, 256), dtype=np.float32)
    wuq = shared["mla_w_uq"]
    sw = np.arange(64).reshape(2, 2, 16)[:, ::-1, :].reshape(64)
    shared["mla_w_uq_sw"] = np.ascontiguousarray(
        np.concatenate([wuq[:, h * 192 + 128 + sw] for h in range(8)], axis=1))
    for k, v in c.items():
        shared["c_" + k] = v
    in_maps = []
    for core in range(NCORES):
        m = dict(shared)
        xs = np.concatenate([inp["x_sample"][core], inp["x_prompt"][2 * core], inp["x_prompt"][2 * core + 1]], axis=0)
        m["xs"] = np.ascontiguousarray(xs, dtype=np.float32)
        cond = np.stack([inp["c"][core], inp["c_ctx"]], axis=0).astype(np.float32)
        crow = cond.reshape(2, 8, 128).transpose(1, 0, 2).reshape(16, 128)
        m["vecs"] = np.ascontiguousarray(np.concatenate([np.stack(rows, axis=0), crow], axis=0), dtype=np.float32)
        m["cckv"] = np.ascontiguousarray(inp["cache_ckv"][core, 0], dtype=np.float32)
        m["ckpe"] = np.ascontiguousarray(inp["cache_kpe"][core, 0], dtype=np.float32)
        in_maps.append(m)
    return in_maps


def kernel(**inputs):
    prog = _get_prog()
    in_maps = _make_in_maps(inputs)
    res = run_bass_kernel_spmd(prog.nc, in_maps, core_ids=list(range(NCORES)))
    y_prompt = np.zeros((16, LP, D), np.float32)
    y_sample = np.zeros((8, LS, D), np.float32)
    new_ckv = np.zeros((16, 1, LP, 256), np.float32)
    new_kpe = np.zeros((16, 1, LP, 64), np.float32)
    for core in range(NCORES):
        r = res.results[core]
        y = r["y"]
        y_sample[core] = y[0:LS]
        y_prompt[2 * core] = y[LS:LS + LP]
        y_prompt[2 * core + 1] = y[LS + LP:T]
        new_ckv[2 * core, 0] = r["nckv"][0:LP]
        new_ckv[2 * core + 1, 0] = r["nckv"][LP:2 * LP]
        new_kpe[2 * core, 0] = r["nkpe"][0:LP]
        new_kpe[2 * core + 1, 0] = r["nkpe"][LP:2 * LP]
    return (y_prompt, y_sample, new_ckv, new_kpe)
```
